# Optimizing a Trainium2 kernel written in Bass

```python
import jax, jax.numpy as jnp
from jax import lax
import numpy as np

D_MODEL = 1024
BATCH = 8
SEQ = 2048
DEPTH = 2
DEC_BATCH = 128
DEC_SEQ = 1
PAST_LEN = 16384
PAGE_SIZE = 128

D_MIX = D_MODEL
D_POOL = D_MIX // 4
POOL_WINDOWS = (2, 4, 8, 16)
N_POOL_GROUPS = len(POOL_WINDOWS)
POOL_GROUP = D_POOL // N_POOL_GROUPS
POOL_BUF = max(POOL_WINDOWS) - 1
D_CONV = D_MIX // 4
CONV_WIDTH = 31
CONV_BUF = CONV_WIDTH - 1
D_GMLP = D_MIX - D_POOL - D_CONV
GMLP_HEAD = 64
N_GMLP_HEADS = D_GMLP // GMLP_HEAD
CHUNK = 128
D_IN = D_POOL + 2 * D_CONV + 2 * D_GMLP
D_FF = -(-(8 * D_MODEL) // (3 * 256)) * 256
N_MOD = 6
EPS = 1e-6

kernel_name = "hybrid_pool_conv_gmlp_decoder_step"


def rms_norm(x, g):
    xf = x.astype(jnp.float32)
    y = xf * lax.rsqrt(jnp.mean(xf * xf, axis=-1, keepdims=True) + EPS)
    return (y * g.astype(jnp.float32)).astype(x.dtype)


def layer_norm(x, g, b):
    xf = x.astype(jnp.float32)
    mu = jnp.mean(xf, axis=-1, keepdims=True)
    var = jnp.mean(jnp.square(xf - mu), axis=-1, keepdims=True)
    y = (xf - mu) * lax.rsqrt(var + EPS) * g.astype(jnp.float32) + b.astype(jnp.float32)
    return y.astype(x.dtype)


def pool_mixer(xa, buf, start, w_grp, scale):
    n, L, _ = xa.shape
    ext = jnp.concatenate([buf.astype(xa.dtype), xa], axis=1)
    cs = jnp.cumsum(ext.astype(jnp.float32), axis=1)
    cs = jnp.pad(cs, ((0, 0), (1, 0), (0, 0)))
    pos = start + jnp.arange(L)
    means = []
    for gi, w in enumerate(POOL_WINDOWS):
        sl = slice(gi * POOL_GROUP, (gi + 1) * POOL_GROUP)
        wsum = cs[:, POOL_BUF + 1:POOL_BUF + 1 + L, sl] - cs[:, POOL_BUF + 1 - w:POOL_BUF + 1 - w + L, sl]
        cnt = jnp.minimum(pos + 1, w).astype(jnp.float32)[None, :, None]
        means.append(wsum / cnt)
    mean = jnp.stack(means, axis=2)
    d = mean - xa.astype(jnp.float32).reshape(n, L, N_POOL_GROUPS, POOL_GROUP)
    y = jnp.einsum('nlgp,gpq->nlgq', d, w_grp.astype(jnp.float32)).reshape(n, L, D_POOL)
    y = y * scale.astype(jnp.float32)
    return y.astype(xa.dtype), ext[:, -POOL_BUF:]


def conv_module(a, gt, buf, w_dw, b_dw, ln_g, ln_b):
    glu = a * jax.nn.sigmoid(gt)
    ext = jnp.concatenate([buf.astype(glu.dtype), glu], axis=1)
    y = lax.conv_general_dilated(ext, w_dw[:, None, :].astype(ext.dtype), window_strides=(1,),
                                 padding='VALID', dimension_numbers=('NWC', 'WIO', 'NWC'),
                                 feature_group_count=D_CONV) + b_dw
    y = jax.nn.silu(layer_norm(y, ln_g, ln_b))
    return y, ext[:, -CONV_BUF:]


def chunk_gmlp(u, v, w_s, b_s, ln_g, ln_b):
    v = layer_norm(v, ln_g, ln_b)
    n, L, _ = v.shape
    Lp = -(-L // CHUNK) * CHUNK
    vp = jnp.pad(v, ((0, 0), (0, Lp - L), (0, 0))).reshape(n, Lp // CHUNK, CHUNK, N_GMLP_HEADS, GMLP_HEAD)
    mask = jnp.tril(jnp.ones((CHUNK, CHUNK), dtype=bool))
    ws = jnp.where(mask[None], w_s, jnp.zeros_like(w_s)).astype(vp.dtype)
    z = jnp.einsum('hij,ncjhd->ncihd', ws, vp) + b_s.T[None, None, :, :, None]
    z = z.reshape(n, Lp, D_GMLP)[:, :L]
    return u * z, v


def trunk_layer(x, c, pool_buf, conv_buf, start, w_ada, b_ada, norm1_g, norm2_g, w_in, pool_w,
                pool_scale, conv_w, conv_b, conv_ln_g, conv_ln_b, gmlp_ln_g, gmlp_ln_b, gmlp_ws,
                gmlp_bs, w_out, w_ff1, w_ff3, w_ff2):
    mod = jax.nn.silu(c) @ w_ada + b_ada
    sh1, sc1, g1, sh2, sc2, g2 = jnp.split(mod[:, None, :], N_MOD, axis=-1)
    h = rms_norm(x, norm1_g) * (1 + sc1) + sh1
    p = h @ w_in
    xa, a, gt, u, v = jnp.split(p, [D_POOL, D_POOL + D_CONV, D_POOL + 2 * D_CONV,
                                    D_POOL + 2 * D_CONV + D_GMLP], axis=-1)
    ya, new_pool = pool_mixer(xa, pool_buf, start, pool_w, pool_scale)
    yb, new_conv = conv_module(a, gt, conv_buf, conv_w, conv_b, conv_ln_g, conv_ln_b)
    yc, v_rows = chunk_gmlp(jax.nn.gelu(u), jax.nn.gelu(v), gmlp_ws, gmlp_bs, gmlp_ln_g, gmlp_ln_b)
    y = jnp.concatenate([ya, yb, yc], axis=-1) @ w_out
    x = x + g1 * y
    h = rms_norm(x, norm2_g) * (1 + sc2) + sh2
    ff = (jax.nn.silu(h @ w_ff1) * (h @ w_ff3)) @ w_ff2
    x = x + g2 * ff
    return x, new_pool, new_conv, v_rows


def setup_inputs(seed: int = 0) -> dict:
    key = jax.random.key(seed)
    ks = jax.random.split(key, 32)
    f32 = jnp.float32
    nrm = lambda k, shape, s: (jax.random.normal(k, shape, f32) * s)
    D = D_MODEL
    return {
        "x_prompt": nrm(ks[0], (BATCH, SEQ, D), 1.0),
        "x_sample": nrm(ks[1], (DEC_BATCH, DEC_SEQ, D), 1.0),
        "c_prompt": nrm(ks[2], (BATCH, D), 1.0),
        "c_sample": nrm(ks[3], (DEC_BATCH, D), 1.0),
        "state_pool": nrm(ks[4], (DEPTH, DEC_BATCH, POOL_BUF, D_POOL), 1.0),
        "state_conv": nrm(ks[5], (DEPTH, DEC_BATCH, CONV_BUF, D_CONV), 0.5),
        "w_ada": nrm(ks[6], (DEPTH, D, N_MOD * D), D ** -0.5),
        "b_ada": nrm(ks[7], (DEPTH, N_MOD * D), 0.02),
        "norm1_g": 1.0 + nrm(ks[8], (DEPTH, D), 0.05),
        "norm2_g": 1.0 + nrm(ks[9], (DEPTH, D), 0.05),
        "w_in": nrm(ks[10], (DEPTH, D, D_IN), D ** -0.5),
        "pool_w": nrm(ks[11], (DEPTH, N_POOL_GROUPS, POOL_GROUP, POOL_GROUP), POOL_GROUP ** -0.5),
        "pool_scale": 1.0 + nrm(ks[12], (DEPTH, D_POOL), 0.1),
        "conv_w": nrm(ks[13], (DEPTH, CONV_WIDTH, D_CONV), CONV_WIDTH ** -0.5),
        "conv_b": nrm(ks[14], (DEPTH, D_CONV), 0.02),
        "conv_ln_g": 1.0 + nrm(ks[15], (DEPTH, D_CONV), 0.05),
        "conv_ln_b": nrm(ks[16], (DEPTH, D_CONV), 0.02),
        "gmlp_ln_g": 1.0 + nrm(ks[17], (DEPTH, D_GMLP), 0.05),
        "gmlp_ln_b": nrm(ks[18], (DEPTH, D_GMLP), 0.02),
        "gmlp_ws": nrm(ks[19], (DEPTH, N_GMLP_HEADS, CHUNK, CHUNK), CHUNK ** -0.5),
        "gmlp_bs": 1.0 + nrm(ks[20], (DEPTH, N_GMLP_HEADS, CHUNK), 0.1),
        "w_out": nrm(ks[21], (DEPTH, D_MIX, D), D_MIX ** -0.5),
        "w_ff1": nrm(ks[22], (DEPTH, D, D_FF), D ** -0.5),
        "w_ff3": nrm(ks[23], (DEPTH, D, D_FF), D ** -0.5),
        "w_ff2": nrm(ks[24], (DEPTH, D_FF, D), D_FF ** -0.5),
        "final_g": 1.0 + nrm(ks[25], (D,), 0.05),
    }


def reference(x_prompt, x_sample, c_prompt, c_sample, state_pool, state_conv, w_ada, b_ada,
              norm1_g, norm2_g, w_in, pool_w, pool_scale, conv_w, conv_b, conv_ln_g, conv_ln_b,
              gmlp_ln_g, gmlp_ln_b, gmlp_ws, gmlp_bs, w_out, w_ff1, w_ff3, w_ff2, final_g):
    xp, xs = x_prompt, x_sample
    nb = xp.shape[0]
    zero_pool = jnp.zeros((nb, POOL_BUF, D_POOL), xp.dtype)
    zero_conv = jnp.zeros((nb, CONV_BUF, D_CONV), xp.dtype)
    pool_p, conv_p, pool_s, conv_s, v_s = [], [], [], [], []
    for l in range(DEPTH):
        w = (w_ada[l], b_ada[l], norm1_g[l], norm2_g[l], w_in[l], pool_w[l], pool_scale[l],
             conv_w[l], conv_b[l], conv_ln_g[l], conv_ln_b[l], gmlp_ln_g[l], gmlp_ln_b[l],
             gmlp_ws[l], gmlp_bs[l], w_out[l], w_ff1[l], w_ff3[l], w_ff2[l])
        xp, npool_p, nconv_p, _ = trunk_layer(xp, c_prompt, zero_pool, zero_conv, 0, *w)
        xs, npool_s, nconv_s, nv_s = trunk_layer(xs, c_sample, state_pool[l], state_conv[l], PAST_LEN, *w)
        pool_p.append(npool_p)
        conv_p.append(nconv_p)
        pool_s.append(npool_s)
        conv_s.append(nconv_s)
        v_s.append(nv_s)
    y_prompt = rms_norm(xp, final_g)
    y_sample = rms_norm(xs, final_g)
    new_pool_prompt = jnp.stack(pool_p)
    new_conv_prompt = jnp.stack(conv_p)
    new_pool_sample = jnp.stack(pool_s)
    new_conv_sample = jnp.stack(conv_s)
    new_chunkv_sample = jnp.stack(v_s)
    return (y_prompt, y_sample, new_pool_prompt, new_conv_prompt, new_pool_sample, new_conv_sample, new_chunkv_sample)
```

```python
import os
import numpy as np
import concourse.bass as bass
import concourse.mybir as mybir
from concourse.bass_utils import run_bass_kernel_spmd
from contextlib import ExitStack

F32 = mybir.dt.float32
BF16 = mybir.dt.bfloat16
AF = mybir.ActivationFunctionType
ALU = mybir.AluOpType
AX = mybir.AxisListType

NCORES = 8
D = 1024
SEQ = 2048
TT = 512
NPT = SEQ // TT
NS = 16
DEPTH = 2
DFF = 2816
NFC = DFF // 128
HALF_FC = (10, 12)
EPS = 1e-6
RING = 7
NA_BLK = 57
ADA0, WIN0, WOUT0, FF10, FF30 = 0, 24, 31, 35, 46
VL = 134
V_N1G, V_N2G, V_BADA, V_PSC, V_CB, V_CLG, V_CLB, V_CW = 0, 8, 16, 64, 66, 68, 70, 72
V_FING = 2 * VL
V_INVW = V_FING + 8
NV = V_INVW + 2
POOL_W = (2, 4, 8, 16)


class Buf:
    __slots__ = ("name", "w", "r")

    def __init__(self, name):
        self.name = name
        self.w = None
        self.r = []


class Chan:
    def __init__(self, name):
        self.name = name
        self.count = 0
        self.sem = None


class Op:
    __slots__ = ("eng", "fn", "deps", "sig", "seq", "chan", "chan_val")


class Prog:
    ENG = ("pe", "act", "dve", "pool", "sp")

    def __init__(self):
        self.ops = {e: [] for e in self.ENG}
        self.chans = []

    def chan(self, name):
        c = Chan(name)
        self.chans.append(c)
        return c

    def op(self, eng, fn, reads=(), writes=(), chan=None):
        o = Op()
        o.eng, o.fn, o.deps, o.sig, o.seq, o.chan, o.chan_val = eng, fn, set(), False, 0, chan, 0
        if chan is not None:
            chan.count += 1
            o.chan_val = chan.count * 16
        for b in reads:
            if b.w is not None:
                self._dep(o, b.w, True)
        for b in writes:
            if b.w is not None:
                self._dep(o, b.w, False)
            for r in b.r:
                self._dep(o, r, False)
        for b in reads:
            b.r.append(o)
        for b in writes:
            b.w = o
            b.r = []
        self.ops[eng].append(o)
        return o

    @staticmethod
    def _dep(o, p, raw):
        if p is o:
            return
        if p.eng == o.eng and p.chan is None and o.chan is None:
            if o.eng == "pe" or (not raw and o.eng != "pool"):
                return
        o.deps.add(p)

    def emit(self, nc, block, esems):
        for e in self.ENG:
            for o in self.ops[e]:
                for p in o.deps:
                    p.sig = True
        for e in self.ENG:
            n = 0
            for o in self.ops[e]:
                if o.chan is None and o.sig:
                    n += 1
                    o.seq = n

        def run(ename, eng):
            waited = {}
            for o in self.ops[ename]:
                need = {}
                for p in o.deps:
                    if p.chan is not None:
                        key, s, v = "c:" + p.chan.name, p.chan.sem, p.chan_val
                    else:
                        key, s, v = "e:" + p.eng, esems[p.eng], p.seq
                    if v > need.get(key, (None, 0))[1]:
                        need[key] = (s, v)
                for key, (s, v) in need.items():
                    if waited.get(key, 0) < v:
                        eng.wait_ge(s, v)
                        waited[key] = v
                ins = o.fn(eng)
                if o.chan is not None:
                    ins.then_inc(o.chan.sem, 16)
                elif o.sig:
                    ins.then_inc(esems[ename], 1)
            if ename == "sp":
                for c in self.chans:
                    if c.count:
                        eng.wait_ge(c.sem, c.count * 16)

        block.tensor(lambda e: run("pe", e))
        block.scalar(lambda e: run("act", e))
        block.vector(lambda e: run("dve", e))
        block.gpsimd(lambda e: run("pool", e))
        block.sync(lambda e: run("sp", e))


def build_nc():
    nc = bass.Bass("TRN2", target_bir_lowering=False)
    P = Prog()
    es = ExitStack()

    def din(name, shape):
        return nc.dram_tensor(name, list(shape), F32, kind="ExternalInput").ap()

    def dout(name, shape):
        return nc.dram_tensor(name, list(shape), F32, kind="ExternalOutput").ap()

    xT_d = din("xT", [128, 8, SEQ])
    xsT_d = din("xsT", [128, 8, NS])
    cT_d = din("cT", [128, 8, 17])
    spool_d = din("spool", [DEPTH, 128, 2, NS, 15])
    sconv_d = din("sconv", [DEPTH, 128, 2, NS, 30])
    wA_d = [din(f"wA{l}", [NA_BLK, 128, 2048]) for l in range(DEPTH)]
    wB_d = [din(f"wB{l}", [16, 128, 1536]) for l in range(DEPTH)]
    vecs_d = din("vecs", [128, NV])
    poolw_d = din("poolw", [128, DEPTH * 2 * 128])
    gws_d = din("gws", [128, DEPTH * 8 * 128])
    maskT_d = din("maskT", [128, 128])
    ident_d = din("ident", [128, 128])
    r16_d = din("r16", [128, 2, 16])
    bsr_d = din("bsr", [2, DEPTH * 4 * 128])
    bsrs_d = din("bsrs", [2, DEPTH * 4 * NS])
    sel_d = din("sel", [2, 128])
    ws00_d = din("ws00", [NS, DEPTH * 8])
    lngb_d = din("lngb", [DEPTH, 128, 2, 512])

    yT_d = dout("yT", [128, 8, SEQ])
    ysT_d = dout("ysT", [128, 8, NS])
    npp_d = dout("npp", [DEPTH, 128, 2, 15])
    ncp_d = dout("ncp", [DEPTH, 128, 2, 30])
    nps_d = dout("nps", [DEPTH, 128, 2, NS, 15])
    ncs_d = dout("ncs", [DEPTH, 128, 2, NS, 30])
    nvs_d = dout("nvs", [DEPTH, NS, 512])

    DBG = bool(os.environ.get("MK_DBG"))
    if DBG:
        dbg_d = {k: dout(k, [128, 8, SEQ]) for k in ("dbg_ycat0", "dbg_ycat1", "dbg_xmix0", "dbg_xmix1", "dbg_xffn0")}
        ch_dbg = P.chan("dbg")
    scA_d = [nc.dram_tensor(f"scA{l}", [33, 128, 2048], BF16, kind="Internal").ap() for l in range(DEPTH)]
    scB_d = [nc.dram_tensor(f"scB{l}", [16, 128, 1536], BF16, kind="Internal").ap() for l in range(DEPTH)]

    def sb(name, shape, dt=F32):
        return es.enter_context(nc.sbuf_tensor(name, list(shape), dt))

    x_res = sb("x_res", [128, 8, TT])
    hT = sb("hT", [128, 8, TT], BF16)
    rstd = sb("rstd", [128, TT])
    NTMP = 6
    tmp32 = [sb(f"tmp32_{i}", [128, TT]) for i in range(NTMP)]
    NBFR = 4
    bfr = [sb(f"bfr_{i}", [128, TT], BF16) for i in range(NBFR)]
    mod = [sb(f"mod{l}", [128, 48, 17]) for l in range(DEPTH)]
    acoef = [[sb(f"acoef{l}_{k}", [128, 8, 17]) for k in range(2)] for l in range(DEPTH)]
    xa_ext = sb("xa_ext", [128, 2, 15 + TT])
    xcarry = sb("xcarry", [128, DEPTH, 2, 15])
    S2b = sb("S2b", [128, 2, 15 + TT])
    S4b = sb("S4b", [128, 2, 15 + TT])
    d_bf = sb("d_bf", [128, 2, TT], BF16)
    poolw_bf = sb("poolw_bf", [128, DEPTH * 2 * 128], BF16)
    ycat = sb("ycat", [128, 8, TT], BF16)
    glu_ext = [sb(f"glu_ext{l}", [128, 2, 30 + TT], BF16) for l in range(DEPTH)]
    gtail = sb("gtail", [128, 2, 30])
    convD = sb("convD", [128, 2, 31, 128], BF16)
    yconv = sb("yconv", [128, 2, TT])
    cm = sb("cm", [128, TT])
    cvar = sb("cvar", [128, TT])
    u_bf = sb("u_bf", [128, 4, TT], BF16)
    vn_bf = sb("vn_bf", [128, 4, 512], BF16)
    vn32s = sb("vn32s", [NS, 512])
    vns_bf = sb("vns_bf", [NS, 512], BF16)
    vstat = sb("vstat", [128, 5, 10])
    lngb = sb("lngb_sb", [128, 2, 512])
    WsT_bf = sb("WsT_bf", [128, DEPTH * 8 * 128], BF16)
    g_ffn = sb("g_ffn", [128, 12, TT], BF16)
    wring = [sb(f"wring{i}", [128, 2048], BF16) for i in range(RING)]
    NSTG, NSUB = 8, 4
    wstg_all = sb("wstg_all", [128, NSTG * 512])
    wstg = [wstg_all[:, i * 512:(i + 1) * 512] for i in range(NSTG)]
    bsr = wstg_all[0:2, 2560:3584]
    bs_tmp = wstg_all[0:2, 3584:4096]
    vecs = sb("vecs_sb", [128, NV])
    ident_f = sb("ident_f", [128, 128])
    ident_bf = sb("ident_bf", [128, 128], BF16)
    ones_bf = sb("ones_bf", [128, 128], BF16)
    maskT = sb("maskT_sb", [128, 128])
    r16 = sb("r16_sb", [128, 2, 16])
    bsrs = sb("bsrs_sb", [2, DEPTH * 4 * NS])
    bs_hi = sb("bs_hi", [2, DEPTH * 4 * 128], BF16)
    bs_lo = sb("bs_lo", [2, DEPTH * 4 * 128], BF16)
    bss_hi = sb("bss_hi", [2, DEPTH * 4 * NS], BF16)
    bss_lo = sb("bss_lo", [2, DEPTH * 4 * NS], BF16)
    sel_f = sb("sel_f", [2, 128])
    sel_bf = sb("sel_bf", [2, 128], BF16)
    ws00 = sb("ws00_sb", [NS, DEPTH * 8])
    rhsS = sb("rhsS", [NS, DEPTH * 8, NS], BF16)
    cT = sb("cT_sb", [128, 8, 17])
    cs_bf = sb("cs_bf", [128, 8, 17], BF16)
    xa_s = sb("xa_s", [128, 2, NS])
    glu_s = sb("glu_s", [128, 2, NS])
    st_pool = sb("st_pool", [128, 2, NS, 15])
    st_conv = sb("st_conv", [128, 2, NS, 30])
    prod_s = sb("prod_s", [128, 2, NS, 30])
    red_s = sb("red_s", [128, 2, NS])
    wsum_s = sb("wsum_s", [128, 2, NS])
    nps_st = sb("nps_st", [128, 2, NS, 15])
    ncs_st = sb("ncs_st", [128, 2, NS, 30])
    npp_st = sb("npp_st", [128, 2, 15])

    if os.environ.get("MK_VERBOSE"):
        print("SBUF bytes remaining per partition:", nc.sbuf_bytes_remaining)
    psum = [es.enter_context(nc.psum_tensor(f"ps{i}", [128, 512], F32)) for i in range(8)]

    B = Buf
    psB = [B(f"ps{i}") for i in range(8)]
    ps_ctr = [0]

    ps_held = set()

    def ps_next(hold=False):
        while True:
            i = ps_ctr[0] % 8
            ps_ctr[0] += 1
            if i not in ps_held:
                break
        if hold:
            ps_held.add(i)
        return psum[i], psB[i]

    def ps_release(pb):
        ps_held.discard(psB.index(pb))

    tmp_ctr = [0]
    tmpB = [B(f"tmp{i}") for i in range(NTMP)]

    def tmp_next():
        i = tmp_ctr[0] % NTMP
        tmp_ctr[0] += 1
        return tmp32[i], tmpB[i]

    bfr_ctr = [0]
    bfrB = [B(f"bfr{i}") for i in range(NBFR)]

    def bfr_next():
        i = bfr_ctr[0] % NBFR
        bfr_ctr[0] += 1
        return bfr[i], bfrB[i]

    XBS = [[B(f"x{i}_{c}") for c in range(8)] for i in range(2)]
    xbufs = [x_res[:, :, :], wstg_all[:, :].rearrange("p (c t) -> p c t", c=8)]
    CUR = {"x": xbufs[0], "XB": XBS[0]}
    HB = [B(f"h{c}") for c in range(8)]
    RSTD = B("rstd")
    MODB = [[B(f"mod{l}_{g}") for g in range(4)] for l in range(DEPTH)]
    ACB = [[B(f"ac{l}_{k}") for k in range(2)] for l in range(DEPTH)]
    XAH, XAB = B("xah"), [B("xab0"), B("xab1")]
    XCAR = [B("xcar0"), B("xcar1")]
    S2B, S4B = B("S2"), B("S4")
    DBF = [B("dbf0"), B("dbf1")]
    YC = [B(f"ycat{c}") for c in range(8)]
    GEH = [B("geh0"), B("geh1")]
    GEB = [[B(f"geb{l}_{c}") for c in range(2)] for l in range(DEPTH)]
    GTAIL = B("gtail")
    CONVD = B("convD")
    YCV = [B("ycv0"), B("ycv1")]
    CM, CVAR = B("cm"), B("cvar")
    UB = [B(f"u{c}") for c in range(4)]
    VNB = [B(f"vn{c}") for c in range(4)]
    VN32S = B("vn32s")
    VST = [B(f"vst{c}") for c in range(5)]
    VNSB = B("vnsb")
    HBS = [B(f"hs{c}") for c in range(8)]
    LNGB = B("lngb")
    GF = [B(f"gf{c}") for c in range(12)]
    YST = [B("yst0"), B("yst1")]
    RINGQ = [[B(f"ring{i}_{q}") for q in range(4)] for i in range(RING)]
    STGB = [B(f"stg{i}") for i in range(8)]
    CONST = B("const")
    C2 = {k: B(k) for k in ("ident_bf", "ones", "poolw_bf", "WsT", "bs", "bss", "sel", "rhsS", "cs")}
    XAS, GLUS, STP, STC, PRODS, REDS, WSUMS, NPSST, NCSST, NPPST = (B(n) for n in (
        "xas", "glus", "stp", "stc", "prods", "reds", "wsums", "npsst", "ncsst", "nppst"))
    SCR = {}

    ch_const = P.chan("const")
    ch_x = [P.chan(f"x{c}") for c in range(8)]
    ch_ring = [P.chan(f"ring{i}") for i in range(RING)]
    ch_ringst = [P.chan(f"ringst{i}") for i in range(RING)]
    ch_stg = [P.chan(f"stg{i}") for i in range(8)]
    ch_yst = [P.chan(f"yst{i}") for i in range(8)]
    ch_misc = {k: P.chan(k) for k in ("lngb", "stp", "stc", "nps", "ncs", "npp", "ncp", "nvs")}

    def vcol(col, n=1, p0=0, p1=128):
        return vecs[p0:p1, col:col + n]

    const_ops = []
    for (dst, src) in ((vecs, vecs_d), (ident_f, ident_d), (maskT, maskT_d), (r16, r16_d),
                       (bsrs, bsrs_d), (sel_f, sel_d), (ws00, ws00_d), (cT, cT_d)):
        const_ops.append(P.op("sp", (lambda d, s: lambda e: e.dma_start(out=d[:], in_=s))(dst, src), chan=ch_const))
    const_ops.append(P.op("sp", lambda e: e.dma_start(out=bsr, in_=bsr_d), chan=ch_const))
    const_ops.append(P.op("sp", lambda e: e.dma_start(out=wstg_all[:, 0:2048], in_=gws_d), chan=ch_const))
    const_ops.append(P.op("sp", lambda e: e.dma_start(out=wstg_all[:, 2048:2048 + DEPTH * 256], in_=poolw_d),
                          chan=ch_const))
    CONST.w = const_ops[-1]
    for b_ in STGB:
        b_.w = const_ops[-1]

    P.op("dve", lambda e: e.tensor_copy(ident_bf[:], ident_f[:]), reads=[CONST], writes=[C2["ident_bf"]])
    P.op("pool", lambda e: e.memset(ones_bf[:], 1.0), writes=[C2["ones"]])
    P.op("dve", lambda e: e.tensor_copy(sel_bf[:], sel_f[:]), reads=[CONST], writes=[C2["sel"]])
    P.op("dve", lambda e: e.tensor_copy(poolw_bf[:], wstg_all[:, 2048:2048 + DEPTH * 256]), reads=[STGB[4]],
         writes=[C2["poolw_bf"]])
    for i in range(DEPTH * 8):
        P.op("dve", (lambda i: lambda e: e.tensor_tensor(WsT_bf[:, i * 128:(i + 1) * 128],
                                                        wstg_all[:, i * 128:(i + 1) * 128], maskT[:], ALU.mult))(i),
             reads=STGB[0:4] + [CONST], writes=[C2["WsT"]])
    P.op("dve", lambda e: e.tensor_copy(bs_hi[:], bsr), reads=STGB[5:7], writes=[C2["bs"]])
    for hf in range(2):
        P.op("dve", (lambda hf: lambda e: e.tensor_tensor(bs_tmp, bsr[:, hf * 512:(hf + 1) * 512],
                                                         bs_hi[:, hf * 512:(hf + 1) * 512], ALU.subtract))(hf),
             reads=STGB[5:7] + [C2["bs"]], writes=[STGB[7]])
        P.op("dve", (lambda hf: lambda e: e.tensor_copy(bs_lo[:, hf * 512:(hf + 1) * 512], bs_tmp))(hf),
             reads=[STGB[7]], writes=[C2["bs"]])
    P.op("dve", lambda e: e.tensor_copy(bss_hi[:], bsrs[:]), reads=[CONST], writes=[C2["bss"]])
    P.op("dve", lambda e: e.tensor_tensor(bs_tmp[:, 0:DEPTH * 4 * NS], bsrs[:], bss_hi[:], ALU.subtract),
         reads=[CONST, C2["bss"]], writes=[STGB[7]])
    P.op("dve", lambda e: e.tensor_copy(bss_lo[:], bs_tmp[:, 0:DEPTH * 4 * NS]), reads=[STGB[7]],
         writes=[C2["bss"]])
    for i in range(DEPTH * 8):
        P.op("dve", (lambda i: lambda e: e.tensor_scalar(rhsS[:, i, :], ident_f[0:NS, 0:NS], ws00[:, i:i + 1], None,
                                                        ALU.mult))(i), reads=[CONST], writes=[C2["rhsS"]])
    P.op("act", lambda e: e.activation(out=cs_bf[:], in_=cT[:], func=AF.Silu), reads=[CONST], writes=[C2["cs"]])

    tiles = [(512, 0, 0), (384, NS, 512), (384, 0, 896), (384, 0, 1280), (384, 0, 1664)]
    if os.environ.get("MK_TILES"):
        tiles = tiles[:int(os.environ["MK_TILES"])]
    def ada_sched(l):
        ev = {("start",): [(0, a) for a in range(8)] if l == 0 else []}
        for i in range(7):
            ev[("win", i)] = [(l, 8 + 2 * i), (l, 9 + 2 * i)] if i < 6 else []
        for i in range(4):
            ev[("wout", i)] = [(l, 20 + i)]
        for p in range(11):
            ev[("pair", p)] = [(1, p)] if (l == 0 and p < 8 and DEPTH > 1) else []
        return ev

    seq = []
    for ti, tile in enumerate(tiles):
        first = (ti == 0)
        for l in range(DEPTH):
            def A(kind, j, la=None):
                seq.append((first, l if la is None else la, kind, j))

            def ADA(ev):
                if first:
                    for (la, a) in ada_sched(l)[ev]:
                        A("ada", a, la)
            ADA(("start",))
            for i, j in enumerate((1, 2, 5, 6, 0, 3, 4)):
                A("win", j)
                ADA(("win", i))
            for j in range(4):
                A("wout", j)
                ADA(("wout", j))
            pi = 0
            for j in range(0, 6):
                A("ff1", j)
                A("ff3", j)
                ADA(("pair", pi))
                pi += 1
            for dm in range(8):
                A("ff2", dm)
            for j in range(6, 11):
                A("ff1", j)
                A("ff3", j)
                ADA(("pair", pi))
                pi += 1
            for dm in range(8):
                A("ff2", 8 + dm)
    wstate = {"issued": 0, "used": 0, "nstg": 0}

    def blk_src(l, kind, j):
        if kind == "ada":
            return wA_d[l][ADA0 + j], 2048, None
        if kind == "win":
            return wA_d[l][WIN0 + j], 2048, scA_d[l][j]
        if kind == "wout":
            return wA_d[l][WOUT0 + j], 2048, scA_d[l][7 + j]
        if kind == "ff1":
            return wA_d[l][FF10 + j], 2048, scA_d[l][11 + j]
        if kind == "ff3":
            return wA_d[l][FF30 + j], 2048, scA_d[l][22 + j]
        if kind == "ff2":
            w = HALF_FC[j // 8] * 128
            return wB_d[l][j][:, 0:w], w, scB_d[l][j][:, 0:w]
        raise ValueError(kind)

    def issue_load(n):
        first, l, kind, j = seq[n]
        slot = n % RING
        src, w, scr = blk_src(l, kind, j)
        if first:
            sw = w // NSUB
            for q in range(NSUB):
                s2 = wstate["nstg"] % NSTG
                wstate["nstg"] += 1
                P.op("sp", (lambda s2, q: lambda e: e.dma_start(out=wstg[s2][:, 0:sw],
                                                                in_=src[:, q * sw:(q + 1) * sw]))(s2, q),
                     writes=[STGB[s2]], chan=ch_stg[s2])
                if True:
                    P.op("dve", (lambda s2, q: lambda e: e.tensor_copy(wring[slot][:, q * sw:(q + 1) * sw],
                                                                       wstg[s2][:, 0:sw]))(s2, q),
                         reads=[STGB[s2]], writes=[RINGQ[slot][q]])
                else:
                    P.op("act", (lambda s2, q: lambda e: e.activation(out=wring[slot][:, q * sw:(q + 1) * sw],
                                                                      in_=wstg[s2][:, 0:sw], func=AF.Copy))(s2, q),
                         reads=[STGB[s2]], writes=[RINGQ[slot][q]])
            if scr is not None:
                pending_store[n] = (scr, slot, w, SCR.setdefault((l, kind, j), Buf(f"scr{l}{kind}{j}")))
        else:
            sb_ = SCR[(l, kind, j)]
            P.op("sp", lambda e: e.dma_start(out=wring[slot][:, 0:w], in_=scr), reads=[sb_], writes=RINGQ[slot],
                 chan=ch_ring[slot])

    pending_store = {}

    def next_blk(l, kind, j):
        n = wstate["used"]
        if n in pending_store:
            scr_, slot_, w_, sb__ = pending_store.pop(n)
            P.op("sp", lambda e: e.dma_start(out=scr_, in_=wring[slot_][:, 0:w_]), reads=RINGQ[slot_],
                 writes=[sb__], chan=ch_ringst[slot_])
        assert seq[n][1:] == (l, kind, j), (seq[n], l, kind, j)
        while wstate["issued"] < min(len(seq), n + RING - 1):
            issue_load(wstate["issued"])
            wstate["issued"] += 1
        wstate["used"] += 1
        return wring[n % RING], RINGQ[n % RING]

    def mm_group(mms, reads, psb, per=None):
        if per is not None:
            for i, (o, lt, r, st, sp_) in enumerate(mms):
                P.op("pe", (lambda o, lt, r, st, sp_: lambda e: e.matmul(o, lt, r, start=st, stop=sp_))(o, lt, r, st, sp_),
                     reads=(list(reads) if i == 0 else []) + list(per[i]), writes=[psb])
            return None

        def fn(e):
            ins = None
            for (o, lt, r, st, sp_) in mms:
                ins = e.matmul(o, lt, r, start=st, stop=sp_)
            return ins
        return P.op("pe", fn, reads=reads, writes=[psb])

    def ada_one(l, a):
        grp = 0 if a < 8 else (1 if a < 12 else (2 if a < 20 else 3))
        W, WB_ = next_blk(l, "ada", a)
        for cc in range(2):
            q = 2 * a + cc
            pt, pb = ps_next()
            mm_group([(pt[:, 0:17], W[:, kc * 256 + cc * 128: kc * 256 + cc * 128 + 128], cs_bf[:, kc, :],
                       kc == 0, kc == 7) for kc in range(8)], WB_ + [C2["cs"]], pb)
            P.op("act", (lambda pt, q: lambda e: e.activation(out=mod[l][:, q, :], in_=pt[:, 0:17],
                                                              func=AF.Identity,
                                                              bias=vcol(l * VL + V_BADA + q)))(pt, q),
                 reads=[pb, CONST], writes=[MODB[l][grp]])

    def make_acoef(l, k):
        scq = 8 if k == 0 else 32
        gcol = l * VL + (V_N1G if k == 0 else V_N2G)
        grp = 0 if k == 0 else 2
        for c in range(8):
            P.op("dve", (lambda c: lambda e: e.tensor_scalar(acoef[l][k][:, c, :], mod[l][:, scq + c, :],
                                                            vcol(gcol + c), vcol(gcol + c), ALU.mult, ALU.add))(c),
                 reads=[MODB[l][grp], CONST], writes=[ACB[l][k]])

    def stats_rstd(T, n_inv, xc=None, XBc=None):
        xc = CUR["x"] if xc is None else xc
        XBc = CUR["XB"] if XBc is None else XBc
        pt, pb = ps_next()
        for c in range(8):
            sq, sqb = bfr_next()
            P.op("act", (lambda sq, c: lambda e: e.activation(out=sq[:, 0:T], in_=xc[:, c, 0:T],
                                                              func=AF.Square))(sq, c),
                 reads=[XBc[c]], writes=[sqb])
            mm_group([(pt[:, 0:T], ones_bf[:], sq[:, 0:T], c == 0, c == 7)], [sqb, C2["ones"]], pb)
        P.op("act", lambda e: e.activation(out=rstd[:, 0:T], in_=pt[:, 0:T], func=AF.Sqrt, bias=EPS, scale=n_inv),
             reads=[pb], writes=[RSTD])
        P.op("dve", lambda e: e.reciprocal(rstd[:, 0:T], rstd[:, 0:T]), reads=[RSTD], writes=[RSTD])

    def norm_mod(l, k, Tp, Ts):
        Tt = Tp + Ts
        xc, XBc = CUR["x"], CUR["XB"]
        stats_rstd(Tt, 1.0 / D)
        shq = 0 if k == 0 else 24
        grp = 0 if k == 0 else 2
        for c in range(8):
            xn, xnb = tmp_next()
            P.op("dve", (lambda xn, c: lambda e: e.tensor_tensor(xn[:, 0:Tt], xc[:, c, 0:Tt], rstd[:, 0:Tt],
                                                                ALU.mult))(xn, c),
                 reads=[XBc[c], RSTD], writes=[xnb])
            P.op("act", (lambda xn, c: lambda e: e.activation(out=hT[:, c, 0:Tp], in_=xn[:, 0:Tp],
                                                              func=AF.Identity,
                                                              scale=acoef[l][k][:, c, 0:1],
                                                              bias=mod[l][:, shq + c, 0:1]))(xn, c),
                 reads=[xnb, ACB[l][k], MODB[l][grp]], writes=[HB[c]])
            if Ts:
                xs_, xsb = tmp_next()
                P.op("dve", (lambda xn, xs_, c: lambda e: e.tensor_tensor(xs_[:, 0:Ts], xn[:, Tp:Tt],
                                                                         acoef[l][k][:, c, 1:17], ALU.mult))(xn, xs_, c),
                     reads=[xnb, ACB[l][k]], writes=[xsb])
                P.op("dve", (lambda xs_, c: lambda e: e.tensor_tensor(hT[:, c, Tp:Tt], xs_[:, 0:Ts],
                                                                     mod[l][:, shq + c, 1:17], ALU.add))(xs_, c),
                     reads=[xsb, MODB[l][grp]], writes=[HBS[c]])

    def resid_evac(l, pt, pb, oc, gq, grp, Tp, Ts):
        Tt = Tp + Ts
        xc, XBc = CUR["x"], CUR["XB"]
        P.op("dve", lambda e: e.scalar_tensor_tensor(xc[:, oc, 0:Tp], pt[:, 0:Tp], mod[l][:, gq + oc, 0:1],
                                                     xc[:, oc, 0:Tp], ALU.mult, ALU.add),
             reads=[pb, MODB[l][grp], XBc[oc]], writes=[XBc[oc]])
        if Ts:
            t, tb = tmp_next()
            P.op("dve", lambda e: e.tensor_tensor(t[:, 0:Ts], pt[:, Tp:Tt], mod[l][:, gq + oc, 1:17], ALU.mult),
                 reads=[pb, MODB[l][grp]], writes=[tb])
            P.op("dve", lambda e: e.tensor_tensor(xc[:, oc, Tp:Tt], xc[:, oc, Tp:Tt], t[:, 0:Ts], ALU.add),
                 reads=[tb, XBc[oc]], writes=[XBc[oc]])

    def build_convD(l, items):
        for (c, k) in items:
            P.op("dve", (lambda c, k: lambda e: e.tensor_scalar(convD[:, c, k, :], ident_bf[:],
                                                               vcol(l * VL + V_CW + c * 31 + k), None,
                                                               ALU.mult))(c, k),
                 reads=[C2["ident_bf"], CONST], writes=[CONVD])

    CONV_ITEMS = [(c, k) for c in range(2) for k in range(31)]
    build_convD(0, CONV_ITEMS)

    def load_x(ti):
        Tp, Ts, t0 = tiles[ti]
        xc, XBc = xbufs[ti % 2], XBS[ti % 2]
        extra = list(STGB) if ti == 1 else []
        for c in range(8):
            P.op("sp", (lambda c: lambda e: e.dma_start(out=xc[:, c, 0:Tp], in_=xT_d[:, c, t0:t0 + Tp]))(c),
                 writes=[XBc[c]] + extra, chan=ch_x[c])
            if Ts:
                P.op("sp", (lambda c: lambda e: e.dma_start(out=xc[:, c, Tp:Tp + Ts], in_=xsT_d[:, c, :]))(c),
                     writes=[XBc[c]], chan=ch_x[c])

    def emit_final(ti):
        Tp, Ts, t0 = tiles[ti]
        Tt = Tp + Ts
        xc, XBc = xbufs[ti % 2], XBS[ti % 2]
        stats_rstd(Tt, 1.0 / D, xc, XBc)
        for c in range(8):
            P.op("dve", (lambda c: lambda e: e.scalar_tensor_tensor(xc[:, c, 0:Tt], xc[:, c, 0:Tt], vcol(V_FING + c),
                                                                   rstd[:, 0:Tt], ALU.mult, ALU.mult))(c),
                 reads=[XBc[c], RSTD, CONST], writes=[XBc[c]])
            P.op("pool", (lambda c: lambda e: e.dma_start(out=yT_d[:, c, t0:t0 + Tp], in_=xc[:, c, 0:Tp]))(c),
                 reads=[XBc[c]], chan=ch_yst[c])
            if Ts:
                P.op("pool", (lambda c: lambda e: e.dma_start(out=ysT_d[:, c, :], in_=xc[:, c, Tp:Tt]))(c),
                     reads=[XBc[c]], chan=ch_yst[c])

    def run_pass(ti, tile, l):
        Tp, Ts, t0 = tile
        Tt = Tp + Ts
        first = (ti == 0)
        firstp = (t0 == 0)
        lastp = (t0 + Tp == SEQ)
        Tp_prev = tiles[ti - 1][0] if ti > 0 else 0
        vb = l * VL
        ntc = Tp // 128
        HBA = HB + HBS

        CUR["x"], CUR["XB"] = xbufs[ti % 2], XBS[ti % 2]
        if l == 0 and ti <= 1:
            load_x(ti)
        if l == DEPTH - 1 and 1 <= ti < len(tiles) - 1:
            load_x(ti + 1)
        P.op("sp", lambda e: e.dma_start(out=lngb[:], in_=lngb_d[l]), writes=[LNGB], chan=ch_misc["lngb"])
        if Ts:
            P.op("sp", lambda e: e.dma_start(out=st_pool[:], in_=spool_d[l]), writes=[STP], chan=ch_misc["stp"])
            P.op("sp", lambda e: e.dma_start(out=st_conv[:], in_=sconv_d[l]), writes=[STC], chan=ch_misc["stc"])

        sched = ada_sched(l) if first else {}

        def do_ada(ev):
            for (la, a) in sched.get(ev, []):
                ada_one(la, a)
        win_i = [0]
        do_ada(("start",))
        if first:
            make_acoef(l, 0)
        norm_mod(l, 0, Tp, Ts)
        if l == 0 and ti > 0:
            emit_final(ti - 1)

        if firstp:
            P.op("pool", lambda e: e.memset(xa_ext[:, :, 0:15], 0.0), writes=[XAH])
            P.op("pool", lambda e: e.memset(glu_ext[l][:, :, 0:30], 0.0), writes=[GEH[l]])
        else:
            P.op("pool", lambda e: e.tensor_copy(xa_ext[:, :, 0:15], xcarry[:, l, :, :]), reads=[XCAR[l]],
                 writes=[XAH])
            P.op("pool", lambda e: e.tensor_copy(glu_ext[l][:, :, 0:30], glu_ext[l][:, :, Tp_prev:Tp_prev + 30]),
                 reads=GEB[l], writes=[GEH[l]])

        held = {}

        def fm_block(j):
            W, WB_ = next_blk(l, "win", j)
            for hf in range(2):
                oc = 2 * j + hf
                pt, pb = ps_next(hold=(oc in (2, 3)))
                fmm = [(pt[:, 0:Tt], W[:, kc * 256 + hf * 128: kc * 256 + hf * 128 + 128], hT[:, kc, 0:Tt],
                        kc == 0, kc == 7) for kc in range(8)]
                if oc == 2:
                    mm_group(fmm, WB_, pb, per=[[HB[kc], HBS[kc]] for kc in range(8)])
                else:
                    mm_group(fmm, WB_ + HBA, pb)
                if oc < 2:
                    c = oc
                    P.op("act", (lambda pt, c: lambda e: e.activation(out=xa_ext[:, c, 15:15 + Tp],
                                                                      in_=pt[:, 0:Tp], func=AF.Copy))(pt, c),
                         reads=[pb], writes=[XAB[c]])
                    if Ts:
                        P.op("act", (lambda pt, c: lambda e: e.activation(out=xa_s[:, c, :], in_=pt[:, Tp:Tt],
                                                                          func=AF.Copy))(pt, c),
                             reads=[pb], writes=[XAS])
                elif oc < 4:
                    held[oc - 2] = (pt, pb)
                elif oc < 6:
                    c = oc - 4
                    apt, apb = held[c]
                    sg, sgb = tmp_next()
                    P.op("act", (lambda pt, sg: lambda e: e.activation(out=sg[:, 0:Tt], in_=pt[:, 0:Tt],
                                                                       func=AF.Sigmoid))(pt, sg),
                         reads=[pb], writes=[sgb])
                    P.op("dve", (lambda apt, sg, c: lambda e: e.tensor_tensor(glu_ext[l][:, c, 30:30 + Tp], apt[:, 0:Tp],
                                                                             sg[:, 0:Tp], ALU.mult))(apt, sg, c),
                         reads=[apb, sgb], writes=[GEB[l][c]])
                    if Ts:
                        P.op("dve", (lambda apt, sg, c: lambda e: e.tensor_tensor(glu_s[:, c, :], apt[:, Tp:Tt],
                                                                                 sg[:, Tp:Tt], ALU.mult))(apt, sg, c),
                             reads=[apb, sgb], writes=[GLUS])
                    if lastp:
                        P.op("dve", (lambda apt, sg, c: lambda e: e.tensor_tensor(gtail[:, c, :], apt[:, Tp - 30:Tp],
                                                                                 sg[:, Tp - 30:Tp], ALU.mult))(apt, sg, c),
                             reads=[apb, sgb], writes=[GTAIL])
                    ps_release(apb)
                else:
                    c = oc - 6
                    P.op("act", (lambda pt, c: lambda e: e.activation(out=u_bf[:, c, 0:Tt], in_=pt[:, 0:Tt],
                                                                      func=AF.Gelu_apprx_tanh))(pt, c),
                         reads=[pb], writes=[UB[c]])
            do_ada(("win", win_i[0]))
            win_i[0] += 1


        fm_block(1)
        fm_block(2)

        vchunks = [(tc * 128, 128, tc) for tc in range(ntc)] + ([(Tp, Ts, 4)] if Ts else [])
        vps = [ps_next(hold=True) for _ in vchunks]
        for j in (5, 6):
            W, WB_ = next_blk(l, "win", j)
            for vi, (c0, M, tc) in enumerate(vchunks):
                pt, pb = vps[vi]
                mms = [(pt[0:M, (j - 5) * 256:(j - 4) * 256], hT[:, kc, c0:c0 + M],
                        W[:, kc * 256:(kc + 1) * 256], kc == 0, kc == 7) for kc in range(8)]
                mm_group(mms, WB_ + HBA, pb)
            do_ada(("win", win_i[0]))
            win_i[0] += 1
        vgs = []
        for vi, (c0, M, tc) in enumerate(vchunks):
            pt, pb = vps[vi]
            vg, vgb = tmp_next()
            vgs.append((vg, vgb))
            P.op("act", (lambda pt, vg, M: lambda e: e.activation(out=vg[0:M, :], in_=pt[0:M, :],
                                                                  func=AF.Gelu_apprx_tanh))(pt, vg, M),
                 reads=[pb], writes=[vgb])
            ps_release(pb)
        for vi, (c0, M, tc) in enumerate(vchunks):
            vg, vgb = vgs[vi]
            P.op("dve", (lambda vg, tc, M: lambda e: e.bn_stats(vstat[0:M, tc, 0:6], vg[0:M, :]))(vg, tc, M),
                 reads=[vgb], writes=[VST[tc]])
            P.op("dve", (lambda tc, M: lambda e: e.bn_aggr(vstat[0:M, tc, 6:8], vstat[0:M, tc, 0:6]))(tc, M),
                 reads=[VST[tc]], writes=[VST[tc]])
        for vi, (c0, M, tc) in enumerate(vchunks):
            P.op("act", (lambda tc, M: lambda e: e.activation(out=vstat[0:M, tc, 8:9], in_=vstat[0:M, tc, 7:8],
                                                              func=AF.Sqrt, bias=EPS, scale=1.0))(tc, M),
                 reads=[VST[tc]], writes=[VST[tc]])
        for vi, (c0, M, tc) in enumerate(vchunks):
            vg, vgb = vgs[vi]
            isS = (tc == 4)
            P.op("dve", (lambda tc, M: lambda e: e.reciprocal(vstat[0:M, tc, 9:10], vstat[0:M, tc, 8:9]))(tc, M),
                 reads=[VST[tc]], writes=[VST[tc]])
            P.op("dve", (lambda vg, tc, M: lambda e: e.tensor_scalar(vg[0:M, :], vg[0:M, :], vstat[0:M, tc, 6:7],
                                                                    vstat[0:M, tc, 9:10], ALU.subtract,
                                                                    ALU.mult))(vg, tc, M),
                 reads=[vgb, VST[tc]], writes=[vgb])
            P.op("pool", (lambda vg, M: lambda e: e.tensor_tensor(vg[0:M, :], vg[0:M, :], lngb[0:M, 0, :],
                                                                 ALU.mult))(vg, M),
                 reads=[vgb, LNGB], writes=[vgb])
            if isS:
                P.op("pool", (lambda vg, M: lambda e: e.tensor_tensor(vn32s[:, :], vg[0:M, :], lngb[0:M, 1, :],
                                                                     ALU.add))(vg, M),
                     reads=[vgb, LNGB], writes=[VN32S])
                P.op("pool", lambda e: e.tensor_copy(vns_bf[:, :], vn32s[:, :]), reads=[VN32S], writes=[VNSB])
                P.op("pool", lambda e: e.dma_start(out=nvs_d[l], in_=vn32s[:, :]), reads=[VN32S],
                     chan=ch_misc["nvs"])
            else:
                P.op("pool", (lambda vg, tc: lambda e: e.tensor_tensor(vn_bf[:, tc, :], vg[:, :], lngb[:, 1, :],
                                                                      ALU.add))(vg, tc),
                     reads=[vgb, LNGB], writes=[VNB[tc]])

        if Ts:
            for c in range(2):
                P.op("dve", (lambda c: lambda e: e.tensor_tensor(
                    prod_s[:, c, :, :], st_conv[:, c, :, :],
                    vecs[:, vb + V_CW + c * 31: vb + V_CW + c * 31 + 30].unsqueeze(1).to_broadcast([128, NS, 30]),
                    ALU.mult))(c), reads=[STC, CONST], writes=[PRODS])
            P.op("dve", lambda e: e.tensor_reduce(red_s[:], prod_s[:], AX.X, ALU.add), reads=[PRODS], writes=[REDS])
            for c in range(2):
                P.op("dve", (lambda c: lambda e: e.scalar_tensor_tensor(
                    red_s[:, c, :], glu_s[:, c, :], vcol(vb + V_CW + c * 31 + 30), red_s[:, c, :], ALU.mult,
                    ALU.add))(c), reads=[REDS, GLUS, CONST], writes=[REDS])
            P.op("pool", lambda e: e.tensor_copy(ncs_st[:, :, :, 0:29], st_conv[:, :, :, 1:30]), reads=[STC],
                 writes=[NCSST])
            P.op("pool", lambda e: e.tensor_copy(ncs_st[:, :, :, 29], glu_s[:]), reads=[GLUS], writes=[NCSST])
            P.op("pool", lambda e: e.dma_start(out=ncs_d[l], in_=ncs_st[:]), reads=[NCSST], chan=ch_misc["ncs"])
        for c in range(2):
            pt, pb = ps_next()
            mm_group([(pt[:, 0:Tp], convD[:, c, k, :], glu_ext[l][:, c, k:k + Tp], k == 0, k == 30)
                      for k in range(31)], [CONVD, GEH[l], GEB[l][c]], pb)
            P.op("act", (lambda pt, c: lambda e: e.activation(out=yconv[:, c, 0:Tp], in_=pt[:, 0:Tp],
                                                              func=AF.Identity, bias=vcol(vb + V_CB + c)))(pt, c),
                 reads=[pb, CONST], writes=[YCV[c]])
            if Ts:
                P.op("act", (lambda c: lambda e: e.activation(out=yconv[:, c, Tp:Tt], in_=red_s[:, c, :],
                                                              func=AF.Identity, bias=vcol(vb + V_CB + c)))(c),
                     reads=[REDS, CONST], writes=[YCV[c]])
        if lastp:
            P.op("pool", lambda e: e.dma_start(out=ncp_d[l], in_=gtail[:]), reads=[GTAIL], chan=ch_misc["ncp"])
        ybs = []
        for c in range(2):
            yb_, ybb = bfr_next()
            ys_, ysb = bfr_next()
            P.op("dve", (lambda yb_, c: lambda e: e.tensor_copy(yb_[:, 0:Tt], yconv[:, c, 0:Tt]))(yb_, c),
                 reads=[YCV[c]], writes=[ybb])
            P.op("act", (lambda ys_, c: lambda e: e.activation(out=ys_[:, 0:Tt], in_=yconv[:, c, 0:Tt],
                                                               func=AF.Square))(ys_, c),
                 reads=[YCV[c]], writes=[ysb])
            ybs.append((yb_, ybb, ys_, ysb))

        fm_block(0)

        if Ts:
            for c in range(2):
                for hf in range(2):
                    w = POOL_W[2 * c + hf]
                    p0, p1 = hf * 64, hf * 64 + 64
                    P.op("dve", (lambda c, p0, p1, w: lambda e: e.tensor_reduce(
                        wsum_s[p0:p1, c, :], st_pool[p0:p1, c, :, 15 - (w - 1):15], AX.X, ALU.add))(c, p0, p1, w),
                        reads=[STP], writes=[WSUMS])
            P.op("dve", lambda e: e.tensor_tensor(wsum_s[:], wsum_s[:], xa_s[:], ALU.add), reads=[WSUMS, XAS],
                 writes=[WSUMS])
            P.op("pool", lambda e: e.tensor_copy(nps_st[:, :, :, 0:14], st_pool[:, :, :, 1:15]), reads=[STP],
                 writes=[NPSST])
            P.op("pool", lambda e: e.tensor_copy(nps_st[:, :, :, 14], xa_s[:]), reads=[XAS], writes=[NPSST])
            P.op("pool", lambda e: e.dma_start(out=nps_d[l], in_=nps_st[:]), reads=[NPSST], chan=ch_misc["nps"])
        E = xa_ext
        L = 15 + Tp
        P.op("pool", lambda e: e.tensor_tensor(S2b[:, :, 1:L], E[:, :, 1:L], E[:, :, 0:L - 1], ALU.add),
             reads=[XAH] + XAB, writes=[S2B])
        P.op("pool", lambda e: e.tensor_tensor(S4b[:, :, 3:L], S2b[:, :, 3:L], S2b[:, :, 1:L - 2], ALU.add),
             reads=[S2B], writes=[S4B])
        P.op("pool", lambda e: e.tensor_tensor(S2b[:, 1, 7:L], S4b[:, 1, 7:L], S4b[:, 1, 3:L - 4], ALU.add),
             reads=[S4B, S2B], writes=[S2B])
        P.op("pool", lambda e: e.tensor_tensor(S4b[64:128, 1, 15:L], S2b[64:128, 1, 15:L],
                                               S2b[64:128, 1, 7:L - 8], ALU.add),
             reads=[S2B, S4B], writes=[S4B])
        if lastp:
            P.op("pool", lambda e: e.tensor_copy(npp_st[:], E[:, :, Tp:Tp + 15]), reads=XAB, writes=[NPPST])
            P.op("pool", lambda e: e.dma_start(out=npp_d[l], in_=npp_st[:]), reads=[NPPST], chan=ch_misc["npp"])
        else:
            P.op("pool", lambda e: e.tensor_copy(xcarry[:, l, :, :], E[:, :, Tp:Tp + 15]), reads=XAB,
                 writes=[XCAR[l]])

        fm_block(3)
        fm_block(4)

        ptS, pbS = ps_next()
        ptQ, pbQ = ps_next()
        mm_group([(ptS[:, 0:Tt], ones_bf[:], ybs[c][0][:, 0:Tt], c == 0, c == 1) for c in range(2)],
                 [ybs[0][1], ybs[1][1], C2["ones"]], pbS)
        mm_group([(ptQ[:, 0:Tt], ones_bf[:], ybs[c][2][:, 0:Tt], c == 0, c == 1) for c in range(2)],
                 [ybs[0][3], ybs[1][3], C2["ones"]], pbQ)
        P.op("act", lambda e: e.activation(out=cm[:, 0:Tt], in_=ptS[:, 0:Tt], func=AF.Identity, scale=1.0 / 256),
             reads=[pbS], writes=[CM])
        msq, msqb = tmp_next()
        P.op("dve", lambda e: e.tensor_tensor(msq[:, 0:Tt], cm[:, 0:Tt], cm[:, 0:Tt], ALU.mult), reads=[CM],
             writes=[msqb])
        P.op("dve", lambda e: e.scalar_tensor_tensor(cvar[:, 0:Tt], ptQ[:, 0:Tt], 1.0 / 256, msq[:, 0:Tt], ALU.mult,
                                                     ALU.subtract), reads=[pbQ, msqb], writes=[CVAR])
        P.op("act", lambda e: e.activation(out=cvar[:, 0:Tt], in_=cvar[:, 0:Tt], func=AF.Sqrt, bias=EPS, scale=1.0),
             reads=[CVAR], writes=[CVAR])
        t1s = []
        for c in range(2):
            t1, t1b = tmp_next()
            t1s.append((t1, t1b))
            P.op("dve", (lambda t1, c: lambda e: e.tensor_tensor(t1[:, 0:Tt], yconv[:, c, 0:Tt], cm[:, 0:Tt],
                                                                ALU.subtract))(t1, c),
                 reads=[YCV[c], CM], writes=[t1b])
        P.op("dve", lambda e: e.reciprocal(cvar[:, 0:Tt], cvar[:, 0:Tt]), reads=[CVAR], writes=[CVAR])
        for c in range(2):
            t1, t1b = t1s[c]
            P.op("dve", (lambda t1: lambda e: e.tensor_tensor(t1[:, 0:Tt], t1[:, 0:Tt], cvar[:, 0:Tt],
                                                             ALU.mult))(t1),
                 reads=[t1b, CVAR], writes=[t1b])
            P.op("act", (lambda t1, c: lambda e: e.activation(out=ycat[:, 2 + c, 0:Tt], in_=t1[:, 0:Tt],
                                                              func=AF.Silu, scale=vcol(vb + V_CLG + c),
                                                              bias=vcol(vb + V_CLB + c)))(t1, c),
                 reads=[t1b, CONST], writes=[YC[2 + c]])

        for hp in range(4):
            pt, pb = ps_next()
            mms = []
            o0 = (l * 4 + hp) * 128
            o0s = (l * 4 + hp) * NS
            for (hi_, his_) in ((bs_hi, bss_hi), (bs_lo, bss_lo)):
                for ci in range(ntc):
                    mms.append([pt[:, ci * 128:(ci + 1) * 128], sel_bf[0:2, :], hi_[0:2, o0:o0 + 128], False, False])
                if Ts:
                    mms.append([pt[:, Tp:Tt], sel_bf[0:2, :], his_[0:2, o0s:o0s + NS], False, False])
            mms[0][3] = True
            for h2 in range(2):
                h = 2 * hp + h2
                for ci in range(ntc):
                    mms.append([pt[64 * h2:64 * h2 + 64, ci * 128:(ci + 1) * 128],
                                vn_bf[:, ci, h * 64:(h + 1) * 64],
                                WsT_bf[:, (l * 8 + h) * 128:(l * 8 + h + 1) * 128], False, False])
                if Ts:
                    mms.append([pt[64 * h2:64 * h2 + 64, Tp:Tt], vns_bf[0:NS, h * 64:(h + 1) * 64],
                                rhsS[0:NS, l * 8 + h, :], False, False])
                mms[-1][4] = True
            rd = VNB[0:ntc] + [VNSB, C2["WsT"], C2["bs"], C2["bss"], C2["rhsS"], C2["sel"]]
            mm_group([tuple(m) for m in mms], rd, pb)
            P.op("dve", (lambda pt, hp: lambda e: e.tensor_tensor(ycat[:, 4 + hp, 0:Tt], pt[:, 0:Tt],
                                                                 u_bf[:, hp, 0:Tt], ALU.mult))(pt, hp),
                 reads=[pb, UB[hp]], writes=[YC[4 + hp]])

        srcs = {(0, 0): S2b, (0, 1): S4b, (1, 0): S2b, (1, 1): S4b}
        for c in range(2):
            for hf in range(2):
                p0, p1 = hf * 64, hf * 64 + 64
                Sw = srcs[(c, hf)]
                P.op("dve", (lambda c, p0, p1, Sw: lambda e: e.scalar_tensor_tensor(
                    d_bf[p0:p1, c, 0:Tp], Sw[p0:p1, c, 15:L], vcol(V_INVW + c, 1, p0, p1), E[p0:p1, c, 15:L],
                    ALU.mult, ALU.subtract))(c, p0, p1, Sw),
                    reads=[S2B, S4B, XAB[c], CONST], writes=[DBF[c]])
                if firstp:
                    t, tb = tmp_next()
                    P.op("dve", (lambda c, p0, p1, Sw, t: lambda e: e.tensor_tensor(
                        t[p0:p1, 0:16], Sw[p0:p1, c, 15:31], r16[p0:p1, c, :], ALU.mult))(c, p0, p1, Sw, t),
                        reads=[S2B, S4B, CONST], writes=[tb])
                    P.op("dve", (lambda c, p0, p1, t: lambda e: e.tensor_tensor(
                        d_bf[p0:p1, c, 0:16], t[p0:p1, 0:16], E[p0:p1, c, 15:31], ALU.subtract))(c, p0, p1, t),
                        reads=[tb, XAB[c]], writes=[DBF[c]])
        if Ts:
            for c in range(2):
                P.op("dve", (lambda c: lambda e: e.scalar_tensor_tensor(d_bf[:, c, Tp:Tt], wsum_s[:, c, :],
                                                                       vcol(V_INVW + c), xa_s[:, c, :], ALU.mult,
                                                                       ALU.subtract))(c),
                     reads=[WSUMS, XAS, CONST], writes=[DBF[c]])
        for c in range(2):
            pt, pb = ps_next()
            mm_group([(pt[:, 0:Tt], poolw_bf[:, (l * 2 + c) * 128:(l * 2 + c + 1) * 128], d_bf[:, c, 0:Tt], True,
                       True)], [DBF[c], C2["poolw_bf"]], pb)
            P.op("act", (lambda pt, c: lambda e: e.activation(out=ycat[:, c, 0:Tt], in_=pt[:, 0:Tt],
                                                              func=AF.Identity, scale=vcol(vb + V_PSC + c)))(pt, c),
                 reads=[pb, CONST], writes=[YC[c]])

        if DBG:
            P.op("pool", lambda e: e.dma_start(out=dbg_d[f"dbg_ycat{l}"][:, :, t0:t0 + Tp], in_=ycat[:, :, 0:Tp]),
                 reads=YC, chan=ch_dbg)
        for j in range(4):
            W, WB_ = next_blk(l, "wout", j)
            for hf in range(2):
                oc = 2 * j + hf
                pt, pb = ps_next()
                mm_group([(pt[:, 0:Tt], W[:, kc * 256 + hf * 128: kc * 256 + hf * 128 + 128], ycat[:, kc, 0:Tt],
                           kc == 0, kc == 7) for kc in range(8)], WB_ + YC, pb)
                resid_evac(l, pt, pb, oc, 16, 1, Tp, Ts)
            do_ada(("wout", j))

        if DBG:
            P.op("pool", (lambda xc: lambda e: e.dma_start(out=dbg_d[f"dbg_xmix{l}"][:, :, t0:t0 + Tp],
                                                          in_=xc[:, :, 0:Tp]))(CUR["x"]),
                 reads=CUR["XB"], chan=ch_dbg)
        if first:
            make_acoef(l, 1)
        norm_mod(l, 1, Tp, Ts)
        nxt = passes.index((ti, tile, l)) + 1
        conv_todo = list(CONV_ITEMS) if nxt < len(passes) else []
        nxt_l = passes[nxt][2] if nxt < len(passes) else 0

        def ffn_pair(j, slot0, firstgrp):
            W1, W1B = next_blk(l, "ff1", j)
            W3, W3B = next_blk(l, "ff3", j)
            for f in range(2):
                slot = slot0 + f
                p1, p1b = ps_next()
                p3, p3b = ps_next()
                mm1 = [(p1[:, 0:Tt], W1[:, kc * 256 + f * 128: kc * 256 + f * 128 + 128], hT[:, kc, 0:Tt],
                        kc == 0, kc == 7) for kc in range(8)]
                if firstgrp and f == 0:
                    mm_group(mm1, W1B + HBS, p1b, per=[[HB[kc]] for kc in range(8)])
                else:
                    mm_group(mm1, W1B + HBA, p1b)
                mm_group([(p3[:, 0:Tt], W3[:, kc * 256 + f * 128: kc * 256 + f * 128 + 128], hT[:, kc, 0:Tt],
                           kc == 0, kc == 7) for kc in range(8)], W3B + HBA, p3b)
                s_, sb_ = tmp_next()
                P.op("act", (lambda p1, s_: lambda e: e.activation(out=s_[:, 0:Tt], in_=p1[:, 0:Tt],
                                                                   func=AF.Silu))(p1, s_),
                     reads=[p1b], writes=[sb_])
                P.op("dve", (lambda p3, s_, slot: lambda e: e.tensor_tensor(g_ffn[:, slot, 0:Tt], s_[:, 0:Tt],
                                                                           p3[:, 0:Tt], ALU.mult))(p3, s_, slot),
                     reads=[p3b, sb_], writes=[GF[slot]])
                build_convD(nxt_l, conv_todo[:3])
                del conv_todo[:3]

        def ffn_down(half, slots):
            for dm in range(8):
                W2, W2B = next_blk(l, "ff2", half * 8 + dm)
                pt, pb = ps_next()
                n = len(slots)
                mm_group([(pt[:, 0:Tt], W2[:, i * 128:(i + 1) * 128], g_ffn[:, slots[i], 0:Tt], i == 0, i == n - 1)
                          for i in range(n)], W2B + [GF[sl] for sl in slots], pb)
                resid_evac(l, pt, pb, dm, 40, 3, Tp, Ts)

        for j in range(0, 5):
            ffn_pair(j, 2 * j, j == 0)
            do_ada(("pair", j))
        ffn_pair(5, 10, False)
        do_ada(("pair", 5))
        ffn_down(0, list(range(10)))
        for j in range(6, 11):
            ffn_pair(j, 2 * (j - 6), False)
            do_ada(("pair", j))
        ffn_down(1, [10, 11] + list(range(10)))
        build_convD(nxt_l, conv_todo)

        if DBG and l == 0:
            P.op("pool", (lambda xc: lambda e: e.dma_start(out=dbg_d["dbg_xffn0"][:, :, t0:t0 + Tp],
                                                          in_=xc[:, :, 0:Tp]))(CUR["x"]),
                 reads=CUR["XB"], chan=ch_dbg)

    passes = [(ti, tile, l) for ti, tile in enumerate(tiles) for l in range(DEPTH)]
    for (ti, tile, l) in passes:
        run_pass(ti, tile, l)
    emit_final(len(tiles) - 1)
    assert wstate["used"] == len(seq), (wstate, len(seq))

    esems = {e: es.enter_context(nc.semaphore("sem_" + e)) for e in ("pe", "act", "dve", "pool", "sp")}
    for c in P.chans:
        c.sem = es.enter_context(nc.semaphore("ch_" + c.name))
    with nc.allow_non_contiguous_dma(reason="small strided state/output tiles"):
        block = es.enter_context(nc.Block())
        P.emit(nc, block, esems)
        es.close()
    return nc


def _fm(v):
    sh = v.shape
    n = sh[-1] // 128
    r = v.reshape(sh[:-1] + (n, 128))
    return np.ascontiguousarray(np.moveaxis(r, -1, 0))


def _blocksA(W, nblk):
    r = W.reshape(8, 128, nblk, 256)
    return np.ascontiguousarray(r.transpose(2, 1, 0, 3)).reshape(nblk, 128, 2048)


def _host_layout(inp):
    f = lambda k: np.asarray(inp[k], dtype=np.float32)
    x_prompt, x_sample, c_prompt, c_sample = f("x_prompt"), f("x_sample"), f("c_prompt"), f("c_sample")
    state_pool, state_conv = f("state_pool"), f("state_conv")
    shared = {}
    for l in range(DEPTH):
        wA = np.concatenate([
            _blocksA(f("w_ada")[l], 24), _blocksA(f("w_in")[l], 7), _blocksA(f("w_out")[l], 4),
            _blocksA(f("w_ff1")[l], 11), _blocksA(f("w_ff3")[l], 11)], axis=0)
        shared[f"wA{l}"] = wA
        w2 = f("w_ff2")[l].reshape(NFC, 128, 8, 128)
        wB = np.zeros((16, 128, 1536), np.float32)
        fc0 = 0
        for half in range(2):
            nfc = HALF_FC[half]
            for dm in range(8):
                blk = w2[fc0:fc0 + nfc, :, dm, :].transpose(1, 0, 2).reshape(128, nfc * 128)
                wB[half * 8 + dm, :, 0:nfc * 128] = blk
            fc0 += nfc
        shared[f"wB{l}"] = wB
    vecs = np.zeros((128, NV), np.float32)
    for l in range(DEPTH):
        b = l * VL
        vecs[:, b + V_N1G:b + V_N1G + 8] = _fm(f("norm1_g")[l])
        vecs[:, b + V_N2G:b + V_N2G + 8] = _fm(f("norm2_g")[l])
        vecs[:, b + V_BADA:b + V_BADA + 48] = _fm(f("b_ada")[l])
        vecs[:, b + V_PSC:b + V_PSC + 2] = _fm(f("pool_scale")[l])
        vecs[:, b + V_CB:b + V_CB + 2] = _fm(f("conv_b")[l])
        vecs[:, b + V_CLG:b + V_CLG + 2] = _fm(f("conv_ln_g")[l])
        vecs[:, b + V_CLB:b + V_CLB + 2] = _fm(f("conv_ln_b")[l])
        cw = _fm(f("conv_w")[l])
        vecs[:, b + V_CW:b + V_CW + 62] = cw.transpose(0, 2, 1).reshape(128, 62)
    vecs[:, V_FING:V_FING + 8] = _fm(f("final_g"))
    wpp = np.array([[POOL_W[2 * c + (p // 64)] for c in range(2)] for p in range(128)], np.float32)
    vecs[:, V_INVW:V_INVW + 2] = 1.0 / wpp
    shared["vecs"] = vecs
    r16 = np.zeros((128, 2, 16), np.float32)
    for t in range(16):
        r16[:, :, t] = 1.0 / np.minimum(t + 1, wpp)
    shared["r16"] = r16
    poolw = np.zeros((128, DEPTH, 2, 128), np.float32)
    pw = f("pool_w")
    for l in range(DEPTH):
        for g in range(4):
            c, hf = g // 2, g % 2
            poolw[hf * 64:(hf + 1) * 64, l, c, hf * 64:(hf + 1) * 64] = pw[l, g]
    shared["poolw"] = poolw.reshape(128, -1)
    shared["gws"] = np.ascontiguousarray(f("gmlp_ws").transpose(3, 0, 1, 2)).reshape(128, -1)
    shared["maskT"] = np.triu(np.ones((128, 128), np.float32))
    shared["ident"] = np.eye(128, dtype=np.float32)
    bs = f("gmlp_bs")
    bsr = bs.reshape(DEPTH, 4, 2, 128).transpose(2, 0, 1, 3)
    shared["bsr"] = np.ascontiguousarray(bsr).reshape(2, -1)
    bsrs = np.repeat(bsr[:, :, :, 0:1], NS, axis=3)
    shared["bsrs"] = np.ascontiguousarray(bsrs).reshape(2, -1)
    sel = np.zeros((2, 128), np.float32)
    sel[0, :64] = 1
    sel[1, 64:] = 1
    shared["sel"] = sel
    ws00 = f("gmlp_ws")[:, :, 0, 0].reshape(1, DEPTH * 8)
    shared["ws00"] = np.ascontiguousarray(np.repeat(ws00, NS, axis=0))
    lngb = np.stack([f("gmlp_ln_g"), f("gmlp_ln_b")], axis=1)
    shared["lngb"] = np.ascontiguousarray(np.broadcast_to(lngb[:, None], (DEPTH, 128, 2, 512)))
    maps = []
    for i in range(NCORES):
        m = dict(shared)
        m["xT"] = np.ascontiguousarray(x_prompt[i].reshape(SEQ, 8, 128).transpose(2, 1, 0))
        xs = x_sample[i * NS:(i + 1) * NS, 0, :]
        m["xsT"] = np.ascontiguousarray(xs.reshape(NS, 8, 128).transpose(2, 1, 0))
        cc = np.concatenate([c_prompt[i:i + 1], c_sample[i * NS:(i + 1) * NS]], axis=0)
        m["cT"] = np.ascontiguousarray(cc.reshape(17, 8, 128).transpose(2, 1, 0))
        sp = state_pool[:, i * NS:(i + 1) * NS]
        m["spool"] = np.ascontiguousarray(sp.reshape(DEPTH, NS, 15, 2, 128).transpose(0, 4, 3, 1, 2))
        sc = state_conv[:, i * NS:(i + 1) * NS]
        m["sconv"] = np.ascontiguousarray(sc.reshape(DEPTH, NS, 30, 2, 128).transpose(0, 4, 3, 1, 2))
        maps.append(m)
    return maps


_NC_CACHE = {}


def kernel(**inputs):
    maps = _host_layout(inputs)
    if "nc" not in _NC_CACHE:
        _NC_CACHE["nc"] = build_nc()
    nc = _NC_CACHE["nc"]
    res = run_bass_kernel_spmd(nc, maps, core_ids=list(range(NCORES)))
    R = res.results
    if os.environ.get("MK_DBG"):
        _NC_CACHE["dbg"] = {k: R[0][k] for k in R[0] if k.startswith("dbg_")}
    y_prompt = np.stack([R[i]["yT"].transpose(2, 1, 0).reshape(SEQ, D) for i in range(NCORES)])
    y_sample = np.concatenate([R[i]["ysT"].transpose(2, 1, 0).reshape(NS, 1, D) for i in range(NCORES)])
    npp = np.stack([R[i]["npp"].transpose(0, 3, 2, 1).reshape(DEPTH, 15, 256) for i in range(NCORES)], axis=1)
    ncp = np.stack([R[i]["ncp"].transpose(0, 3, 2, 1).reshape(DEPTH, 30, 256) for i in range(NCORES)], axis=1)
    nps = np.concatenate([R[i]["nps"].transpose(0, 3, 4, 2, 1).reshape(DEPTH, NS, 15, 256) for i in range(NCORES)],
                         axis=1)
    ncs = np.concatenate([R[i]["ncs"].transpose(0, 3, 4, 2, 1).reshape(DEPTH, NS, 30, 256) for i in range(NCORES)],
                         axis=1)
    nvs = np.concatenate([R[i]["nvs"].reshape(DEPTH, NS, 1, 512) for i in range(NCORES)], axis=1)
    f32 = lambda a: np.ascontiguousarray(a, dtype=np.float32)
    return (f32(y_prompt), f32(y_sample), f32(npp), f32(ncp), f32(nps), f32(ncs), f32(nvs))
```

```python
import os
import numpy as np
import concourse.bass as bass
import concourse.mybir as mybir
from concourse.bass_utils import run_bass_kernel_spmd
from contextlib import ExitStack

F32 = mybir.dt.float32
BF16 = mybir.dt.bfloat16
AF = mybir.ActivationFunctionType
ALU = mybir.AluOpType
AX = mybir.AxisListType

NCORES = 8
D = 1024
SEQ = 2048
TT = 512
NPT = SEQ // TT
NS = 16
DEPTH = 2
DFF = 2816
NFC = DFF // 128
HALF_FC = (10, 12)
EPS = 1e-6
RING = 7
NA_BLK = 57
ADA0, WIN0, WOUT0, FF10, FF30 = 0, 24, 31, 35, 46
VL = 134
V_N1G, V_N2G, V_BADA, V_PSC, V_CB, V_CLG, V_CLB, V_CW = 0, 8, 16, 64, 66, 68, 70, 72
V_FING = 2 * VL
V_INVW = V_FING + 8
NV = V_INVW + 2
POOL_W = (2, 4, 8, 16)


class Buf:
    __slots__ = ("name", "w", "r")

    def __init__(self, name):
        self.name = name
        self.w = None
        self.r = []


class Chan:
    def __init__(self, name):
        self.name = name
        self.count = 0
        self.sem = None


class Op:
    __slots__ = ("eng", "fn", "deps", "sig", "seq", "chan", "chan_val")


class Prog:
    ENG = ("pe", "act", "dve", "pool", "sp")

    def __init__(self):
        self.ops = {e: [] for e in self.ENG}
        self.chans = []

    def chan(self, name):
        c = Chan(name)
        self.chans.append(c)
        return c

    def op(self, eng, fn, reads=(), writes=(), chan=None):
        o = Op()
        o.eng, o.fn, o.deps, o.sig, o.seq, o.chan, o.chan_val = eng, fn, set(), False, 0, chan, 0
        if chan is not None:
            chan.count += 1
            o.chan_val = chan.count * 16
        for b in reads:
            if b.w is not None:
                self._dep(o, b.w, True)
        for b in writes:
            if b.w is not None:
                self._dep(o, b.w, False)
            for r in b.r:
                self._dep(o, r, False)
        for b in reads:
            b.r.append(o)
        for b in writes:
            b.w = o
            b.r = []
        self.ops[eng].append(o)
        return o

    @staticmethod
    def _dep(o, p, raw):
        if p is o:
            return
        if p.eng == o.eng and p.chan is None and o.chan is None:
            if o.eng == "pe" or (not raw and o.eng != "pool"):
                return
        o.deps.add(p)

    def emit(self, nc, block, esems):
        for e in self.ENG:
            for o in self.ops[e]:
                for p in o.deps:
                    p.sig = True
        for e in self.ENG:
            n = 0
            for o in self.ops[e]:
                if o.chan is None and o.sig:
                    n += 1
                    o.seq = n

        def run(ename, eng):
            waited = {}
            for o in self.ops[ename]:
                need = {}
                for p in o.deps:
                    if p.chan is not None:
                        key, s, v = "c:" + p.chan.name, p.chan.sem, p.chan_val
                    else:
                        key, s, v = "e:" + p.eng, esems[p.eng], p.seq
                    if v > need.get(key, (None, 0))[1]:
                        need[key] = (s, v)
                for key, (s, v) in need.items():
                    if waited.get(key, 0) < v:
                        eng.wait_ge(s, v)
                        waited[key] = v
                ins = o.fn(eng)
                if o.chan is not None:
                    ins.then_inc(o.chan.sem, 16)
                elif o.sig:
                    ins.then_inc(esems[ename], 1)
            if ename == "sp":
                for c in self.chans:
                    if c.count:
                        eng.wait_ge(c.sem, c.count * 16)

        block.tensor(lambda e: run("pe", e))
        block.scalar(lambda e: run("act", e))
        block.vector(lambda e: run("dve", e))
        block.gpsimd(lambda e: run("pool", e))
        block.sync(lambda e: run("sp", e))


def build_nc():
    nc = bass.Bass("TRN2", target_bir_lowering=False)
    P = Prog()
    es = ExitStack()

    def din(name, shape):
        return nc.dram_tensor(name, list(shape), F32, kind="ExternalInput").ap()

    def dout(name, shape):
        return nc.dram_tensor(name, list(shape), F32, kind="ExternalOutput").ap()

    xT_d = din("xT", [128, 8, SEQ])
    xsT_d = din("xsT", [128, 8, NS])
    cT_d = din("cT", [128, 8, 17])
    spool_d = din("spool", [DEPTH, 128, 2, NS, 15])
    sconv_d = din("sconv", [DEPTH, 128, 2, NS, 30])
    wA_d = [din(f"wA{l}", [NA_BLK, 128, 2048]) for l in range(DEPTH)]
    wB_d = [din(f"wB{l}", [16, 128, 1536]) for l in range(DEPTH)]
    vecs_d = din("vecs", [128, NV])
    poolw_d = din("poolw", [128, DEPTH * 2 * 128])
    gws_d = din("gws", [128, DEPTH * 8 * 128])
    maskT_d = din("maskT", [128, 128])
    ident_d = din("ident", [128, 128])
    r16_d = din("r16", [128, 2, 16])
    bsr_d = din("bsr", [2, DEPTH * 4 * 128])
    bsrs_d = din("bsrs", [2, DEPTH * 4 * NS])
    sel_d = din("sel", [2, 128])
    ws00_d = din("ws00", [NS, DEPTH * 8])
    lngb_d = din("lngb", [DEPTH, 128, 2, 512])

    yT_d = dout("yT", [128, 8, SEQ])
    ysT_d = dout("ysT", [128, 8, NS])
    npp_d = dout("npp", [DEPTH, 128, 2, 15])
    ncp_d = dout("ncp", [DEPTH, 128, 2, 30])
    nps_d = dout("nps", [DEPTH, 128, 2, NS, 15])
    ncs_d = dout("ncs", [DEPTH, 128, 2, NS, 30])
    nvs_d = dout("nvs", [DEPTH, NS, 512])

    DBG = bool(os.environ.get("MK_DBG"))
    if DBG:
        dbg_d = {k: dout(k, [128, 8, SEQ]) for k in ("dbg_ycat0", "dbg_ycat1", "dbg_xmix0", "dbg_xmix1", "dbg_xffn0")}
        ch_dbg = P.chan("dbg")
    scA_d = [nc.dram_tensor(f"scA{l}", [33, 128, 2048], BF16, kind="Internal").ap() for l in range(DEPTH)]
    scB_d = [nc.dram_tensor(f"scB{l}", [16, 128, 1536], BF16, kind="Internal").ap() for l in range(DEPTH)]

    def sb(name, shape, dt=F32):
        return es.enter_context(nc.sbuf_tensor(name, list(shape), dt))

    x_res = sb("x_res", [128, 8, TT])
    hT = sb("hT", [128, 8, TT], BF16)
    rstd = sb("rstd", [128, TT])
    NTMP = 8
    tmp32 = [sb(f"tmp32_{i}", [128, TT]) for i in range(NTMP)]
    NBFR = 4
    bfr = [sb(f"bfr_{i}", [128, TT], BF16) for i in range(NBFR)]
    mod = [sb(f"mod{l}", [128, 48, 17]) for l in range(DEPTH)]
    acoef = [[sb(f"acoef{l}_{k}", [128, 8, 17]) for k in range(2)] for l in range(DEPTH)]
    xa_ext = sb("xa_ext", [128, 2, 15 + TT])
    xcarry = sb("xcarry", [128, DEPTH, 2, 15])
    S2b = sb("S2b", [128, 2, 15 + TT])
    S4b = sb("S4b", [128, 2, 15 + TT])
    d_bf = sb("d_bf", [128, 2, TT], BF16)
    poolw_bf = sb("poolw_bf", [128, DEPTH * 2 * 128], BF16)
    ycat = sb("ycat", [128, 8, TT], BF16)
    glu_ext = [sb(f"glu_ext{l}", [128, 2, 30 + TT], BF16) for l in range(DEPTH)]
    gtail = sb("gtail", [128, 2, 30])
    convD = sb("convD", [128, 2, 31, 128], BF16)
    yconv = sb("yconv", [128, 2, TT])
    cm = sb("cm", [128, TT])
    cvar = sb("cvar", [128, TT])
    u_bf = sb("u_bf", [128, 4, TT], BF16)
    vn_bf = sb("vn_bf", [128, 4, 512], BF16)
    vn32s = sb("vn32s", [NS, 512])
    vns_bf = sb("vns_bf", [NS, 512], BF16)
    vstat = sb("vstat", [128, 5, 10])
    lngb = sb("lngb_sb", [128, 2, 512])
    WsT_bf = sb("WsT_bf", [128, DEPTH * 8 * 128], BF16)
    g_ffn = sb("g_ffn", [128, 12, TT], BF16)
    wring = [sb(f"wring{i}", [128, 2048], BF16) for i in range(RING)]
    NSTG, NSUB = 8, 4
    wstg_all = sb("wstg_all", [128, NSTG * 512])
    wstg = [wstg_all[:, i * 512:(i + 1) * 512] for i in range(NSTG)]
    bsr = wstg_all[0:2, 2560:3584]
    bs_tmp = wstg_all[0:2, 3584:4096]
    vecs = sb("vecs_sb", [128, NV])
    ident_f = sb("ident_f", [128, 128])
    ident_bf = sb("ident_bf", [128, 128], BF16)
    ones_bf = sb("ones_bf", [128, 128], BF16)
    maskT = sb("maskT_sb", [128, 128])
    r16 = sb("r16_sb", [128, 2, 16])
    bsrs = sb("bsrs_sb", [2, DEPTH * 4 * NS])
    bs_hi = sb("bs_hi", [2, DEPTH * 4 * 128], BF16)
    bs_lo = sb("bs_lo", [2, DEPTH * 4 * 128], BF16)
    bss_hi = sb("bss_hi", [2, DEPTH * 4 * NS], BF16)
    bss_lo = sb("bss_lo", [2, DEPTH * 4 * NS], BF16)
    sel_f = sb("sel_f", [2, 128])
    sel_bf = sb("sel_bf", [2, 128], BF16)
    ws00 = sb("ws00_sb", [NS, DEPTH * 8])
    rhsS = sb("rhsS", [NS, DEPTH * 8, NS], BF16)
    cT = sb("cT_sb", [128, 8, 17])
    cs_bf = sb("cs_bf", [128, 8, 17], BF16)
    xa_s = sb("xa_s", [128, 2, NS])
    glu_s = sb("glu_s", [128, 2, NS])
    st_pool = sb("st_pool", [128, 2, NS, 15])
    st_conv = sb("st_conv", [128, 2, NS, 30])
    prod_s = sb("prod_s", [128, 2, NS, 30])
    red_s = sb("red_s", [128, 2, NS])
    wsum_s = sb("wsum_s", [128, 2, NS])
    nps_st = sb("nps_st", [128, 2, NS, 15])
    ncs_st = sb("ncs_st", [128, 2, NS, 30])
    npp_st = sb("npp_st", [128, 2, 15])

    if os.environ.get("MK_VERBOSE"):
        print("SBUF bytes remaining per partition:", nc.sbuf_bytes_remaining)
    psum = [es.enter_context(nc.psum_tensor(f"ps{i}", [128, 512], F32)) for i in range(8)]

    B = Buf
    psB = [B(f"ps{i}") for i in range(8)]
    ps_ctr = [0]

    ps_held = set()

    def ps_next(hold=False):
        while True:
            i = ps_ctr[0] % 8
            ps_ctr[0] += 1
            if i not in ps_held:
                break
        if hold:
            ps_held.add(i)
        return psum[i], psB[i]

    def ps_release(pb):
        ps_held.discard(psB.index(pb))

    tmp_ctr = [0]
    tmpB = [B(f"tmp{i}") for i in range(NTMP)]

    def tmp_next():
        i = tmp_ctr[0] % NTMP
        tmp_ctr[0] += 1
        return tmp32[i], tmpB[i]

    bfr_ctr = [0]
    bfrB = [B(f"bfr{i}") for i in range(NBFR)]

    def bfr_next():
        i = bfr_ctr[0] % NBFR
        bfr_ctr[0] += 1
        return bfr[i], bfrB[i]

    XBS = [[B(f"x{i}_{c}") for c in range(8)] for i in range(2)]
    xbufs = [x_res[:, :, :], wstg_all[:, :].rearrange("p (c t) -> p c t", c=8)]
    CUR = {"x": xbufs[0], "XB": XBS[0]}
    HB = [B(f"h{c}") for c in range(8)]
    RSTD = B("rstd")
    MODB = [[B(f"mod{l}_{g}") for g in range(4)] for l in range(DEPTH)]
    ACB = [[B(f"ac{l}_{k}") for k in range(2)] for l in range(DEPTH)]
    XAH, XAB = B("xah"), [B("xab0"), B("xab1")]
    XCAR = [B("xcar0"), B("xcar1")]
    S2B, S4B = B("S2"), B("S4")
    DBF = [B("dbf0"), B("dbf1")]
    YC = [B(f"ycat{c}") for c in range(8)]
    GEH = [B("geh0"), B("geh1")]
    GEB = [[B(f"geb{l}_{c}") for c in range(2)] for l in range(DEPTH)]
    GTAIL = B("gtail")
    CONVD = B("convD")
    YCV = [B("ycv0"), B("ycv1")]
    CM, CVAR = B("cm"), B("cvar")
    UB = [B(f"u{c}") for c in range(4)]
    VNB = [B(f"vn{c}") for c in range(4)]
    VN32S = B("vn32s")
    VST = [B(f"vst{c}") for c in range(5)]
    VNSB = B("vnsb")
    HBS = [B(f"hs{c}") for c in range(8)]
    LNGB = B("lngb")
    GF = [B(f"gf{c}") for c in range(12)]
    YST = [B("yst0"), B("yst1")]
    RINGQ = [[B(f"ring{i}_{q}") for q in range(4)] for i in range(RING)]
    STGB = [B(f"stg{i}") for i in range(8)]
    CONST = B("const")
    C2 = {k: B(k) for k in ("ident_bf", "ones", "poolw_bf", "WsT", "bs", "bss", "sel", "rhsS", "cs")}
    XAS, GLUS, STP, STC, PRODS, REDS, WSUMS, NPSST, NCSST, NPPST = (B(n) for n in (
        "xas", "glus", "stp", "stc", "prods", "reds", "wsums", "npsst", "ncsst", "nppst"))
    SCR = {}

    ch_const = P.chan("const")
    ch_x = [P.chan(f"x{c}") for c in range(8)]
    ch_ring = [P.chan(f"ring{i}") for i in range(RING)]
    ch_ringst = [P.chan(f"ringst{i}") for i in range(RING)]
    ch_stg = [P.chan(f"stg{i}") for i in range(8)]
    ch_yst = [P.chan(f"yst{i}") for i in range(8)]
    ch_misc = {k: P.chan(k) for k in ("lngb", "stp", "stc", "nps", "ncs", "npp", "ncp", "nvs")}

    def vcol(col, n=1, p0=0, p1=128):
        return vecs[p0:p1, col:col + n]

    const_ops = []
    for (dst, src) in ((vecs, vecs_d), (ident_f, ident_d), (maskT, maskT_d), (r16, r16_d),
                       (bsrs, bsrs_d), (sel_f, sel_d), (ws00, ws00_d), (cT, cT_d)):
        const_ops.append(P.op("sp", (lambda d, s: lambda e: e.dma_start(out=d[:], in_=s))(dst, src), chan=ch_const))
    const_ops.append(P.op("sp", lambda e: e.dma_start(out=bsr, in_=bsr_d), chan=ch_const))
    const_ops.append(P.op("sp", lambda e: e.dma_start(out=wstg_all[:, 0:2048], in_=gws_d), chan=ch_const))
    const_ops.append(P.op("sp", lambda e: e.dma_start(out=wstg_all[:, 2048:2048 + DEPTH * 256], in_=poolw_d),
                          chan=ch_const))
    CONST.w = const_ops[-1]
    for b_ in STGB:
        b_.w = const_ops[-1]

    P.op("dve", lambda e: e.tensor_copy(ident_bf[:], ident_f[:]), reads=[CONST], writes=[C2["ident_bf"]])
    P.op("pool", lambda e: e.memset(ones_bf[:], 1.0), writes=[C2["ones"]])
    P.op("dve", lambda e: e.tensor_copy(sel_bf[:], sel_f[:]), reads=[CONST], writes=[C2["sel"]])
    P.op("dve", lambda e: e.tensor_copy(poolw_bf[:], wstg_all[:, 2048:2048 + DEPTH * 256]), reads=[STGB[4]],
         writes=[C2["poolw_bf"]])
    for i in range(DEPTH * 8):
        P.op("dve", (lambda i: lambda e: e.tensor_tensor(WsT_bf[:, i * 128:(i + 1) * 128],
                                                        wstg_all[:, i * 128:(i + 1) * 128], maskT[:], ALU.mult))(i),
             reads=STGB[0:4] + [CONST], writes=[C2["WsT"]])
    P.op("dve", lambda e: e.tensor_copy(bs_hi[:], bsr), reads=STGB[5:7], writes=[C2["bs"]])
    for hf in range(2):
        P.op("dve", (lambda hf: lambda e: e.tensor_tensor(bs_tmp, bsr[:, hf * 512:(hf + 1) * 512],
                                                         bs_hi[:, hf * 512:(hf + 1) * 512], ALU.subtract))(hf),
             reads=STGB[5:7] + [C2["bs"]], writes=[STGB[7]])
        P.op("dve", (lambda hf: lambda e: e.tensor_copy(bs_lo[:, hf * 512:(hf + 1) * 512], bs_tmp))(hf),
             reads=[STGB[7]], writes=[C2["bs"]])
    P.op("dve", lambda e: e.tensor_copy(bss_hi[:], bsrs[:]), reads=[CONST], writes=[C2["bss"]])
    P.op("dve", lambda e: e.tensor_tensor(bs_tmp[:, 0:DEPTH * 4 * NS], bsrs[:], bss_hi[:], ALU.subtract),
         reads=[CONST, C2["bss"]], writes=[STGB[7]])
    P.op("dve", lambda e: e.tensor_copy(bss_lo[:], bs_tmp[:, 0:DEPTH * 4 * NS]), reads=[STGB[7]],
         writes=[C2["bss"]])
    for i in range(DEPTH * 8):
        P.op("dve", (lambda i: lambda e: e.tensor_scalar(rhsS[:, i, :], ident_f[0:NS, 0:NS], ws00[:, i:i + 1], None,
                                                        ALU.mult))(i), reads=[CONST], writes=[C2["rhsS"]])
    P.op("act", lambda e: e.activation(out=cs_bf[:], in_=cT[:], func=AF.Silu), reads=[CONST], writes=[C2["cs"]])

    tiles = [(512, 0, 0), (384, NS, 512), (384, 0, 896), (384, 0, 1280), (384, 0, 1664)]
    if os.environ.get("MK_TILES"):
        tiles = tiles[:int(os.environ["MK_TILES"])]
    def ada_sched(l):
        ev = {("start",): [(0, a) for a in range(8)] if l == 0 else []}
        for i in range(7):
            ev[("win", i)] = [(l, 8 + 2 * i), (l, 9 + 2 * i)] if i < 6 else []
        for i in range(4):
            ev[("wout", i)] = [(l, 20 + i)]
        for p in range(11):
            ev[("pair", p)] = [(1, p)] if (l == 0 and p < 8 and DEPTH > 1) else []
        return ev

    seq = []
    for ti, tile in enumerate(tiles):
        first = (ti == 0)
        for l in range(DEPTH):
            def A(kind, j, la=None):
                seq.append((first, l if la is None else la, kind, j))

            def ADA(ev):
                if first:
                    for (la, a) in ada_sched(l)[ev]:
                        A("ada", a, la)
            ADA(("start",))
            for i, j in enumerate((1, 2, 5, 6, 0, 3, 4)):
                A("win", j)
                ADA(("win", i))
            for j in range(4):
                A("wout", j)
                ADA(("wout", j))
            pi = 0
            for j in range(0, 6):
                A("ff1", j)
                A("ff3", j)
                ADA(("pair", pi))
                pi += 1
            for dm in range(8):
                A("ff2", dm)
            for j in range(6, 11):
                A("ff1", j)
                A("ff3", j)
                ADA(("pair", pi))
                pi += 1
            for dm in range(8):
                A("ff2", 8 + dm)
    wstate = {"issued": 0, "used": 0, "nstg": 0}

    def blk_src(l, kind, j):
        if kind == "ada":
            return wA_d[l][ADA0 + j], 2048, None
        if kind == "win":
            return wA_d[l][WIN0 + j], 2048, scA_d[l][j]
        if kind == "wout":
            return wA_d[l][WOUT0 + j], 2048, scA_d[l][7 + j]
        if kind == "ff1":
            return wA_d[l][FF10 + j], 2048, scA_d[l][11 + j]
        if kind == "ff3":
            return wA_d[l][FF30 + j], 2048, scA_d[l][22 + j]
        if kind == "ff2":
            w = HALF_FC[j // 8] * 128
            return wB_d[l][j][:, 0:w], w, scB_d[l][j][:, 0:w]
        raise ValueError(kind)

    def issue_load(n):
        first, l, kind, j = seq[n]
        slot = n % RING
        src, w, scr = blk_src(l, kind, j)
        if first:
            sw = w // NSUB
            for q in range(NSUB):
                s2 = wstate["nstg"] % NSTG
                wstate["nstg"] += 1
                P.op("sp", (lambda s2, q: lambda e: e.dma_start(out=wstg[s2][:, 0:sw],
                                                                in_=src[:, q * sw:(q + 1) * sw]))(s2, q),
                     writes=[STGB[s2]], chan=ch_stg[s2])
                if True:
                    P.op("dve", (lambda s2, q: lambda e: e.tensor_copy(wring[slot][:, q * sw:(q + 1) * sw],
                                                                       wstg[s2][:, 0:sw]))(s2, q),
                         reads=[STGB[s2]], writes=[RINGQ[slot][q]])
                else:
                    P.op("act", (lambda s2, q: lambda e: e.activation(out=wring[slot][:, q * sw:(q + 1) * sw],
                                                                      in_=wstg[s2][:, 0:sw], func=AF.Copy))(s2, q),
                         reads=[STGB[s2]], writes=[RINGQ[slot][q]])
            if scr is not None:
                sb_ = SCR.setdefault((l, kind, j), Buf(f"scr{l}{kind}{j}"))
                P.op("pool", lambda e: e.dma_start(out=scr, in_=wring[slot][:, 0:w]), reads=RINGQ[slot],
                     writes=[sb_], chan=ch_ringst[slot])
        else:
            sb_ = SCR[(l, kind, j)]
            P.op("sp", lambda e: e.dma_start(out=wring[slot][:, 0:w], in_=scr), reads=[sb_], writes=RINGQ[slot],
                 chan=ch_ring[slot])

    def next_blk(l, kind, j):
        n = wstate["used"]
        assert seq[n][1:] == (l, kind, j), (seq[n], l, kind, j)
        while wstate["issued"] < min(len(seq), n + RING - 1):
            issue_load(wstate["issued"])
            wstate["issued"] += 1
        wstate["used"] += 1
        return wring[n % RING], RINGQ[n % RING]

    def mm_group(mms, reads, psb, per=None):
        if per is not None:
            for i, (o, lt, r, st, sp_) in enumerate(mms):
                P.op("pe", (lambda o, lt, r, st, sp_: lambda e: e.matmul(o, lt, r, start=st, stop=sp_))(o, lt, r, st, sp_),
                     reads=(list(reads) if i == 0 else []) + list(per[i]), writes=[psb])
            return None

        def fn(e):
            ins = None
            for (o, lt, r, st, sp_) in mms:
                ins = e.matmul(o, lt, r, start=st, stop=sp_)
            return ins
        return P.op("pe", fn, reads=reads, writes=[psb])

    def ada_one(l, a):
        grp = 0 if a < 8 else (1 if a < 12 else (2 if a < 20 else 3))
        W, WB_ = next_blk(l, "ada", a)
        for cc in range(2):
            q = 2 * a + cc
            pt, pb = ps_next()
            mm_group([(pt[:, 0:17], W[:, kc * 256 + cc * 128: kc * 256 + cc * 128 + 128], cs_bf[:, kc, :],
                       kc == 0, kc == 7) for kc in range(8)], WB_ + [C2["cs"]], pb)
            P.op("act", (lambda pt, q: lambda e: e.activation(out=mod[l][:, q, :], in_=pt[:, 0:17],
                                                              func=AF.Identity,
                                                              bias=vcol(l * VL + V_BADA + q)))(pt, q),
                 reads=[pb, CONST], writes=[MODB[l][grp]])

    def make_acoef(l, k):
        scq = 8 if k == 0 else 32
        gcol = l * VL + (V_N1G if k == 0 else V_N2G)
        grp = 0 if k == 0 else 2
        for c in range(8):
            P.op("dve", (lambda c: lambda e: e.tensor_scalar(acoef[l][k][:, c, :], mod[l][:, scq + c, :],
                                                            vcol(gcol + c), vcol(gcol + c), ALU.mult, ALU.add))(c),
                 reads=[MODB[l][grp], CONST], writes=[ACB[l][k]])

    def stats_rstd(T, n_inv, xc=None, XBc=None):
        xc = CUR["x"] if xc is None else xc
        XBc = CUR["XB"] if XBc is None else XBc
        pt, pb = ps_next()
        for c in range(8):
            sq, sqb = bfr_next()
            P.op("act", (lambda sq, c: lambda e: e.activation(out=sq[:, 0:T], in_=xc[:, c, 0:T],
                                                              func=AF.Square))(sq, c),
                 reads=[XBc[c]], writes=[sqb])
            mm_group([(pt[:, 0:T], ones_bf[:], sq[:, 0:T], c == 0, c == 7)], [sqb, C2["ones"]], pb)
        P.op("act", lambda e: e.activation(out=rstd[:, 0:T], in_=pt[:, 0:T], func=AF.Sqrt, bias=EPS, scale=n_inv),
             reads=[pb], writes=[RSTD])
        P.op("dve", lambda e: e.reciprocal(rstd[:, 0:T], rstd[:, 0:T]), reads=[RSTD], writes=[RSTD])

    def norm_mod(l, k, Tp, Ts):
        Tt = Tp + Ts
        xc, XBc = CUR["x"], CUR["XB"]
        stats_rstd(Tt, 1.0 / D)
        shq = 0 if k == 0 else 24
        grp = 0 if k == 0 else 2
        for c in range(8):
            xn, xnb = tmp_next()
            P.op("dve", (lambda xn, c: lambda e: e.tensor_tensor(xn[:, 0:Tt], xc[:, c, 0:Tt], rstd[:, 0:Tt],
                                                                ALU.mult))(xn, c),
                 reads=[XBc[c], RSTD], writes=[xnb])
            P.op("act", (lambda xn, c: lambda e: e.activation(out=hT[:, c, 0:Tp], in_=xn[:, 0:Tp],
                                                              func=AF.Identity,
                                                              scale=acoef[l][k][:, c, 0:1],
                                                              bias=mod[l][:, shq + c, 0:1]))(xn, c),
                 reads=[xnb, ACB[l][k], MODB[l][grp]], writes=[HB[c]])
            if Ts:
                xs_, xsb = tmp_next()
                P.op("dve", (lambda xn, xs_, c: lambda e: e.tensor_tensor(xs_[:, 0:Ts], xn[:, Tp:Tt],
                                                                         acoef[l][k][:, c, 1:17], ALU.mult))(xn, xs_, c),
                     reads=[xnb, ACB[l][k]], writes=[xsb])
                P.op("dve", (lambda xs_, c: lambda e: e.tensor_tensor(hT[:, c, Tp:Tt], xs_[:, 0:Ts],
                                                                     mod[l][:, shq + c, 1:17], ALU.add))(xs_, c),
                     reads=[xsb, MODB[l][grp]], writes=[HBS[c]])

    def resid_evac(l, pt, pb, oc, gq, grp, Tp, Ts):
        Tt = Tp + Ts
        xc, XBc = CUR["x"], CUR["XB"]
        P.op("dve", lambda e: e.scalar_tensor_tensor(xc[:, oc, 0:Tp], pt[:, 0:Tp], mod[l][:, gq + oc, 0:1],
                                                     xc[:, oc, 0:Tp], ALU.mult, ALU.add),
             reads=[pb, MODB[l][grp], XBc[oc]], writes=[XBc[oc]])
        if Ts:
            t, tb = tmp_next()
            P.op("dve", lambda e: e.tensor_tensor(t[:, 0:Ts], pt[:, Tp:Tt], mod[l][:, gq + oc, 1:17], ALU.mult),
                 reads=[pb, MODB[l][grp]], writes=[tb])
            P.op("dve", lambda e: e.tensor_tensor(xc[:, oc, Tp:Tt], xc[:, oc, Tp:Tt], t[:, 0:Ts], ALU.add),
                 reads=[tb, XBc[oc]], writes=[XBc[oc]])

    def build_convD(l, items):
        for (c, k) in items:
            P.op("dve", (lambda c, k: lambda e: e.tensor_scalar(convD[:, c, k, :], ident_bf[:],
                                                               vcol(l * VL + V_CW + c * 31 + k), None,
                                                               ALU.mult))(c, k),
                 reads=[C2["ident_bf"], CONST], writes=[CONVD])

    CONV_ITEMS = [(c, k) for c in range(2) for k in range(31)]
    build_convD(0, CONV_ITEMS)

    def load_x(ti):
        Tp, Ts, t0 = tiles[ti]
        xc, XBc = xbufs[ti % 2], XBS[ti % 2]
        extra = list(STGB) if ti == 1 else []
        for c in range(8):
            P.op("sp", (lambda c: lambda e: e.dma_start(out=xc[:, c, 0:Tp], in_=xT_d[:, c, t0:t0 + Tp]))(c),
                 writes=[XBc[c]] + extra, chan=ch_x[c])
            if Ts:
                P.op("sp", (lambda c: lambda e: e.dma_start(out=xc[:, c, Tp:Tp + Ts], in_=xsT_d[:, c, :]))(c),
                     writes=[XBc[c]], chan=ch_x[c])

    def emit_final(ti):
        Tp, Ts, t0 = tiles[ti]
        Tt = Tp + Ts
        xc, XBc = xbufs[ti % 2], XBS[ti % 2]
        stats_rstd(Tt, 1.0 / D, xc, XBc)
        for c in range(8):
            P.op("dve", (lambda c: lambda e: e.scalar_tensor_tensor(xc[:, c, 0:Tt], xc[:, c, 0:Tt], vcol(V_FING + c),
                                                                   rstd[:, 0:Tt], ALU.mult, ALU.mult))(c),
                 reads=[XBc[c], RSTD, CONST], writes=[XBc[c]])
            P.op("pool", (lambda c: lambda e: e.dma_start(out=yT_d[:, c, t0:t0 + Tp], in_=xc[:, c, 0:Tp]))(c),
                 reads=[XBc[c]], chan=ch_yst[c])
            if Ts:
                P.op("pool", (lambda c: lambda e: e.dma_start(out=ysT_d[:, c, :], in_=xc[:, c, Tp:Tt]))(c),
                     reads=[XBc[c]], chan=ch_yst[c])

    def run_pass(ti, tile, l):
        Tp, Ts, t0 = tile
        Tt = Tp + Ts
        first = (ti == 0)
        firstp = (t0 == 0)
        lastp = (t0 + Tp == SEQ)
        Tp_prev = tiles[ti - 1][0] if ti > 0 else 0
        vb = l * VL
        ntc = Tp // 128
        HBA = HB + HBS

        CUR["x"], CUR["XB"] = xbufs[ti % 2], XBS[ti % 2]
        if l == 0 and ti <= 1:
            load_x(ti)
        if l == DEPTH - 1 and 1 <= ti < len(tiles) - 1:
            load_x(ti + 1)
        P.op("sp", lambda e: e.dma_start(out=lngb[:], in_=lngb_d[l]), writes=[LNGB], chan=ch_misc["lngb"])
        if Ts:
            P.op("sp", lambda e: e.dma_start(out=st_pool[:], in_=spool_d[l]), writes=[STP], chan=ch_misc["stp"])
            P.op("sp", lambda e: e.dma_start(out=st_conv[:], in_=sconv_d[l]), writes=[STC], chan=ch_misc["stc"])

        sched = ada_sched(l) if first else {}

        def do_ada(ev):
            for (la, a) in sched.get(ev, []):
                ada_one(la, a)
        win_i = [0]
        do_ada(("start",))
        if first:
            make_acoef(l, 0)
        norm_mod(l, 0, Tp, Ts)
        if l == 0 and ti > 0:
            emit_final(ti - 1)

        if firstp:
            P.op("pool", lambda e: e.memset(xa_ext[:, :, 0:15], 0.0), writes=[XAH])
            P.op("pool", lambda e: e.memset(glu_ext[l][:, :, 0:30], 0.0), writes=[GEH[l]])
        else:
            P.op("pool", lambda e: e.tensor_copy(xa_ext[:, :, 0:15], xcarry[:, l, :, :]), reads=[XCAR[l]],
                 writes=[XAH])
            P.op("pool", lambda e: e.tensor_copy(glu_ext[l][:, :, 0:30], glu_ext[l][:, :, Tp_prev:Tp_prev + 30]),
                 reads=GEB[l], writes=[GEH[l]])

        held = {}

        def fm_block(j):
            W, WB_ = next_blk(l, "win", j)
            for hf in range(2):
                oc = 2 * j + hf
                pt, pb = ps_next(hold=(oc in (2, 3)))
                fmm = [(pt[:, 0:Tt], W[:, kc * 256 + hf * 128: kc * 256 + hf * 128 + 128], hT[:, kc, 0:Tt],
                        kc == 0, kc == 7) for kc in range(8)]
                if oc == 2:
                    mm_group(fmm, WB_, pb, per=[[HB[kc], HBS[kc]] for kc in range(8)])
                else:
                    mm_group(fmm, WB_ + HBA, pb)
                if oc < 2:
                    c = oc
                    P.op("act", (lambda pt, c: lambda e: e.activation(out=xa_ext[:, c, 15:15 + Tp],
                                                                      in_=pt[:, 0:Tp], func=AF.Copy))(pt, c),
                         reads=[pb], writes=[XAB[c]])
                    if Ts:
                        P.op("act", (lambda pt, c: lambda e: e.activation(out=xa_s[:, c, :], in_=pt[:, Tp:Tt],
                                                                          func=AF.Copy))(pt, c),
                             reads=[pb], writes=[XAS])
                elif oc < 4:
                    held[oc - 2] = (pt, pb)
                elif oc < 6:
                    c = oc - 4
                    apt, apb = held[c]
                    sg, sgb = tmp_next()
                    P.op("act", (lambda pt, sg: lambda e: e.activation(out=sg[:, 0:Tt], in_=pt[:, 0:Tt],
                                                                       func=AF.Sigmoid))(pt, sg),
                         reads=[pb], writes=[sgb])
                    P.op("dve", (lambda apt, sg, c: lambda e: e.tensor_tensor(glu_ext[l][:, c, 30:30 + Tp], apt[:, 0:Tp],
                                                                             sg[:, 0:Tp], ALU.mult))(apt, sg, c),
                         reads=[apb, sgb], writes=[GEB[l][c]])
                    if Ts:
                        P.op("dve", (lambda apt, sg, c: lambda e: e.tensor_tensor(glu_s[:, c, :], apt[:, Tp:Tt],
                                                                                 sg[:, Tp:Tt], ALU.mult))(apt, sg, c),
                             reads=[apb, sgb], writes=[GLUS])
                    if lastp:
                        P.op("dve", (lambda apt, sg, c: lambda e: e.tensor_tensor(gtail[:, c, :], apt[:, Tp - 30:Tp],
                                                                                 sg[:, Tp - 30:Tp], ALU.mult))(apt, sg, c),
                             reads=[apb, sgb], writes=[GTAIL])
                    ps_release(apb)
                else:
                    c = oc - 6
                    P.op("act", (lambda pt, c: lambda e: e.activation(out=u_bf[:, c, 0:Tt], in_=pt[:, 0:Tt],
                                                                      func=AF.Gelu_apprx_tanh))(pt, c),
                         reads=[pb], writes=[UB[c]])
            do_ada(("win", win_i[0]))
            win_i[0] += 1


        fm_block(1)
        fm_block(2)

        vchunks = [(tc * 128, 128, tc) for tc in range(ntc)] + ([(Tp, Ts, 4)] if Ts else [])
        vps = [ps_next(hold=True) for _ in vchunks]
        for j in (5, 6):
            W, WB_ = next_blk(l, "win", j)
            for vi, (c0, M, tc) in enumerate(vchunks):
                pt, pb = vps[vi]
                mms = [(pt[0:M, (j - 5) * 256:(j - 4) * 256], hT[:, kc, c0:c0 + M],
                        W[:, kc * 256:(kc + 1) * 256], kc == 0, kc == 7) for kc in range(8)]
                mm_group(mms, WB_ + HBA, pb)
            do_ada(("win", win_i[0]))
            win_i[0] += 1
        vgs = []
        for vi, (c0, M, tc) in enumerate(vchunks):
            pt, pb = vps[vi]
            vg, vgb = tmp_next()
            vgs.append((vg, vgb))
            P.op("act", (lambda pt, vg, M: lambda e: e.activation(out=vg[0:M, :], in_=pt[0:M, :],
                                                                  func=AF.Gelu_apprx_tanh))(pt, vg, M),
                 reads=[pb], writes=[vgb])
            ps_release(pb)
        for vi, (c0, M, tc) in enumerate(vchunks):
            vg, vgb = vgs[vi]
            P.op("dve", (lambda vg, tc, M: lambda e: e.bn_stats(vstat[0:M, tc, 0:6], vg[0:M, :]))(vg, tc, M),
                 reads=[vgb], writes=[VST[tc]])
            P.op("dve", (lambda tc, M: lambda e: e.bn_aggr(vstat[0:M, tc, 6:8], vstat[0:M, tc, 0:6]))(tc, M),
                 reads=[VST[tc]], writes=[VST[tc]])
        for vi, (c0, M, tc) in enumerate(vchunks):
            P.op("act", (lambda tc, M: lambda e: e.activation(out=vstat[0:M, tc, 8:9], in_=vstat[0:M, tc, 7:8],
                                                              func=AF.Sqrt, bias=EPS, scale=1.0))(tc, M),
                 reads=[VST[tc]], writes=[VST[tc]])
        for vi, (c0, M, tc) in enumerate(vchunks):
            vg, vgb = vgs[vi]
            isS = (tc == 4)
            P.op("dve", (lambda tc, M: lambda e: e.reciprocal(vstat[0:M, tc, 9:10], vstat[0:M, tc, 8:9]))(tc, M),
                 reads=[VST[tc]], writes=[VST[tc]])
            P.op("dve", (lambda vg, tc, M: lambda e: e.tensor_scalar(vg[0:M, :], vg[0:M, :], vstat[0:M, tc, 6:7],
                                                                    vstat[0:M, tc, 9:10], ALU.subtract,
                                                                    ALU.mult))(vg, tc, M),
                 reads=[vgb, VST[tc]], writes=[vgb])
            P.op("pool", (lambda vg, M: lambda e: e.tensor_tensor(vg[0:M, :], vg[0:M, :], lngb[0:M, 0, :],
                                                                 ALU.mult))(vg, M),
                 reads=[vgb, LNGB], writes=[vgb])
            if isS:
                P.op("pool", (lambda vg, M: lambda e: e.tensor_tensor(vn32s[:, :], vg[0:M, :], lngb[0:M, 1, :],
                                                                     ALU.add))(vg, M),
                     reads=[vgb, LNGB], writes=[VN32S])
                P.op("pool", lambda e: e.tensor_copy(vns_bf[:, :], vn32s[:, :]), reads=[VN32S], writes=[VNSB])
                P.op("pool", lambda e: e.dma_start(out=nvs_d[l], in_=vn32s[:, :]), reads=[VN32S],
                     chan=ch_misc["nvs"])
            else:
                P.op("pool", (lambda vg, tc: lambda e: e.tensor_tensor(vn_bf[:, tc, :], vg[:, :], lngb[:, 1, :],
                                                                      ALU.add))(vg, tc),
                     reads=[vgb, LNGB], writes=[VNB[tc]])

        if Ts:
            for c in range(2):
                P.op("dve", (lambda c: lambda e: e.tensor_tensor(
                    prod_s[:, c, :, :], st_conv[:, c, :, :],
                    vecs[:, vb + V_CW + c * 31: vb + V_CW + c * 31 + 30].unsqueeze(1).to_broadcast([128, NS, 30]),
                    ALU.mult))(c), reads=[STC, CONST], writes=[PRODS])
            P.op("dve", lambda e: e.tensor_reduce(red_s[:], prod_s[:], AX.X, ALU.add), reads=[PRODS], writes=[REDS])
            for c in range(2):
                P.op("dve", (lambda c: lambda e: e.scalar_tensor_tensor(
                    red_s[:, c, :], glu_s[:, c, :], vcol(vb + V_CW + c * 31 + 30), red_s[:, c, :], ALU.mult,
                    ALU.add))(c), reads=[REDS, GLUS, CONST], writes=[REDS])
            P.op("pool", lambda e: e.tensor_copy(ncs_st[:, :, :, 0:29], st_conv[:, :, :, 1:30]), reads=[STC],
                 writes=[NCSST])
            P.op("pool", lambda e: e.tensor_copy(ncs_st[:, :, :, 29], glu_s[:]), reads=[GLUS], writes=[NCSST])
            P.op("pool", lambda e: e.dma_start(out=ncs_d[l], in_=ncs_st[:]), reads=[NCSST], chan=ch_misc["ncs"])
        for c in range(2):
            pt, pb = ps_next()
            mm_group([(pt[:, 0:Tp], convD[:, c, k, :], glu_ext[l][:, c, k:k + Tp], k == 0, k == 30)
                      for k in range(31)], [CONVD, GEH[l], GEB[l][c]], pb)
            P.op("act", (lambda pt, c: lambda e: e.activation(out=yconv[:, c, 0:Tp], in_=pt[:, 0:Tp],
                                                              func=AF.Identity, bias=vcol(vb + V_CB + c)))(pt, c),
                 reads=[pb, CONST], writes=[YCV[c]])
            if Ts:
                P.op("act", (lambda c: lambda e: e.activation(out=yconv[:, c, Tp:Tt], in_=red_s[:, c, :],
                                                              func=AF.Identity, bias=vcol(vb + V_CB + c)))(c),
                     reads=[REDS, CONST], writes=[YCV[c]])
        if lastp:
            P.op("pool", lambda e: e.dma_start(out=ncp_d[l], in_=gtail[:]), reads=[GTAIL], chan=ch_misc["ncp"])
        ybs = []
        for c in range(2):
            yb_, ybb = bfr_next()
            ys_, ysb = bfr_next()
            P.op("dve", (lambda yb_, c: lambda e: e.tensor_copy(yb_[:, 0:Tt], yconv[:, c, 0:Tt]))(yb_, c),
                 reads=[YCV[c]], writes=[ybb])
            P.op("act", (lambda ys_, c: lambda e: e.activation(out=ys_[:, 0:Tt], in_=yconv[:, c, 0:Tt],
                                                               func=AF.Square))(ys_, c),
                 reads=[YCV[c]], writes=[ysb])
            ybs.append((yb_, ybb, ys_, ysb))

        fm_block(0)

        if Ts:
            for c in range(2):
                for hf in range(2):
                    w = POOL_W[2 * c + hf]
                    p0, p1 = hf * 64, hf * 64 + 64
                    P.op("dve", (lambda c, p0, p1, w: lambda e: e.tensor_reduce(
                        wsum_s[p0:p1, c, :], st_pool[p0:p1, c, :, 15 - (w - 1):15], AX.X, ALU.add))(c, p0, p1, w),
                        reads=[STP], writes=[WSUMS])
            P.op("dve", lambda e: e.tensor_tensor(wsum_s[:], wsum_s[:], xa_s[:], ALU.add), reads=[WSUMS, XAS],
                 writes=[WSUMS])
            P.op("pool", lambda e: e.tensor_copy(nps_st[:, :, :, 0:14], st_pool[:, :, :, 1:15]), reads=[STP],
                 writes=[NPSST])
            P.op("pool", lambda e: e.tensor_copy(nps_st[:, :, :, 14], xa_s[:]), reads=[XAS], writes=[NPSST])
            P.op("pool", lambda e: e.dma_start(out=nps_d[l], in_=nps_st[:]), reads=[NPSST], chan=ch_misc["nps"])
        E = xa_ext
        L = 15 + Tp
        P.op("pool", lambda e: e.tensor_tensor(S2b[:, :, 1:L], E[:, :, 1:L], E[:, :, 0:L - 1], ALU.add),
             reads=[XAH] + XAB, writes=[S2B])
        P.op("pool", lambda e: e.tensor_tensor(S4b[:, :, 3:L], S2b[:, :, 3:L], S2b[:, :, 1:L - 2], ALU.add),
             reads=[S2B], writes=[S4B])
        P.op("pool", lambda e: e.tensor_tensor(S2b[:, 1, 7:L], S4b[:, 1, 7:L], S4b[:, 1, 3:L - 4], ALU.add),
             reads=[S4B, S2B], writes=[S2B])
        P.op("pool", lambda e: e.tensor_tensor(S4b[64:128, 1, 15:L], S2b[64:128, 1, 15:L],
                                               S2b[64:128, 1, 7:L - 8], ALU.add),
             reads=[S2B, S4B], writes=[S4B])
        if lastp:
            P.op("pool", lambda e: e.tensor_copy(npp_st[:], E[:, :, Tp:Tp + 15]), reads=XAB, writes=[NPPST])
            P.op("pool", lambda e: e.dma_start(out=npp_d[l], in_=npp_st[:]), reads=[NPPST], chan=ch_misc["npp"])
        else:
            P.op("pool", lambda e: e.tensor_copy(xcarry[:, l, :, :], E[:, :, Tp:Tp + 15]), reads=XAB,
                 writes=[XCAR[l]])

        fm_block(3)
        fm_block(4)

        ptS, pbS = ps_next()
        ptQ, pbQ = ps_next()
        mm_group([(ptS[:, 0:Tt], ones_bf[:], ybs[c][0][:, 0:Tt], c == 0, c == 1) for c in range(2)],
                 [ybs[0][1], ybs[1][1], C2["ones"]], pbS)
        mm_group([(ptQ[:, 0:Tt], ones_bf[:], ybs[c][2][:, 0:Tt], c == 0, c == 1) for c in range(2)],
                 [ybs[0][3], ybs[1][3], C2["ones"]], pbQ)
        P.op("act", lambda e: e.activation(out=cm[:, 0:Tt], in_=ptS[:, 0:Tt], func=AF.Identity, scale=1.0 / 256),
             reads=[pbS], writes=[CM])
        msq, msqb = tmp_next()
        P.op("dve", lambda e: e.tensor_tensor(msq[:, 0:Tt], cm[:, 0:Tt], cm[:, 0:Tt], ALU.mult), reads=[CM],
             writes=[msqb])
        P.op("dve", lambda e: e.scalar_tensor_tensor(cvar[:, 0:Tt], ptQ[:, 0:Tt], 1.0 / 256, msq[:, 0:Tt], ALU.mult,
                                                     ALU.subtract), reads=[pbQ, msqb], writes=[CVAR])
        P.op("act", lambda e: e.activation(out=cvar[:, 0:Tt], in_=cvar[:, 0:Tt], func=AF.Sqrt, bias=EPS, scale=1.0),
             reads=[CVAR], writes=[CVAR])
        t1s = []
        for c in range(2):
            t1, t1b = tmp_next()
            t1s.append((t1, t1b))
            P.op("dve", (lambda t1, c: lambda e: e.tensor_tensor(t1[:, 0:Tt], yconv[:, c, 0:Tt], cm[:, 0:Tt],
                                                                ALU.subtract))(t1, c),
                 reads=[YCV[c], CM], writes=[t1b])
        P.op("dve", lambda e: e.reciprocal(cvar[:, 0:Tt], cvar[:, 0:Tt]), reads=[CVAR], writes=[CVAR])
        for c in range(2):
            t1, t1b = t1s[c]
            P.op("dve", (lambda t1: lambda e: e.tensor_tensor(t1[:, 0:Tt], t1[:, 0:Tt], cvar[:, 0:Tt],
                                                             ALU.mult))(t1),
                 reads=[t1b, CVAR], writes=[t1b])
            P.op("act", (lambda t1, c: lambda e: e.activation(out=ycat[:, 2 + c, 0:Tt], in_=t1[:, 0:Tt],
                                                              func=AF.Silu, scale=vcol(vb + V_CLG + c),
                                                              bias=vcol(vb + V_CLB + c)))(t1, c),
                 reads=[t1b, CONST], writes=[YC[2 + c]])

        for hp in range(4):
            pt, pb = ps_next()
            mms = []
            o0 = (l * 4 + hp) * 128
            o0s = (l * 4 + hp) * NS
            for (hi_, his_) in ((bs_hi, bss_hi), (bs_lo, bss_lo)):
                for ci in range(ntc):
                    mms.append([pt[:, ci * 128:(ci + 1) * 128], sel_bf[0:2, :], hi_[0:2, o0:o0 + 128], False, False])
                if Ts:
                    mms.append([pt[:, Tp:Tt], sel_bf[0:2, :], his_[0:2, o0s:o0s + NS], False, False])
            mms[0][3] = True
            for h2 in range(2):
                h = 2 * hp + h2
                for ci in range(ntc):
                    mms.append([pt[64 * h2:64 * h2 + 64, ci * 128:(ci + 1) * 128],
                                vn_bf[:, ci, h * 64:(h + 1) * 64],
                                WsT_bf[:, (l * 8 + h) * 128:(l * 8 + h + 1) * 128], False, False])
                if Ts:
                    mms.append([pt[64 * h2:64 * h2 + 64, Tp:Tt], vns_bf[0:NS, h * 64:(h + 1) * 64],
                                rhsS[0:NS, l * 8 + h, :], False, False])
                mms[-1][4] = True
            rd = VNB[0:ntc] + [VNSB, C2["WsT"], C2["bs"], C2["bss"], C2["rhsS"], C2["sel"]]
            mm_group([tuple(m) for m in mms], rd, pb)
            P.op("dve", (lambda pt, hp: lambda e: e.tensor_tensor(ycat[:, 4 + hp, 0:Tt], pt[:, 0:Tt],
                                                                 u_bf[:, hp, 0:Tt], ALU.mult))(pt, hp),
                 reads=[pb, UB[hp]], writes=[YC[4 + hp]])

        srcs = {(0, 0): S2b, (0, 1): S4b, (1, 0): S2b, (1, 1): S4b}
        for c in range(2):
            for hf in range(2):
                p0, p1 = hf * 64, hf * 64 + 64
                Sw = srcs[(c, hf)]
                P.op("dve", (lambda c, p0, p1, Sw: lambda e: e.scalar_tensor_tensor(
                    d_bf[p0:p1, c, 0:Tp], Sw[p0:p1, c, 15:L], vcol(V_INVW + c, 1, p0, p1), E[p0:p1, c, 15:L],
                    ALU.mult, ALU.subtract))(c, p0, p1, Sw),
                    reads=[S2B, S4B, XAB[c], CONST], writes=[DBF[c]])
                if firstp:
                    t, tb = tmp_next()
                    P.op("dve", (lambda c, p0, p1, Sw, t: lambda e: e.tensor_tensor(
                        t[p0:p1, 0:16], Sw[p0:p1, c, 15:31], r16[p0:p1, c, :], ALU.mult))(c, p0, p1, Sw, t),
                        reads=[S2B, S4B, CONST], writes=[tb])
                    P.op("dve", (lambda c, p0, p1, t: lambda e: e.tensor_tensor(
                        d_bf[p0:p1, c, 0:16], t[p0:p1, 0:16], E[p0:p1, c, 15:31], ALU.subtract))(c, p0, p1, t),
                        reads=[tb, XAB[c]], writes=[DBF[c]])
        if Ts:
            for c in range(2):
                P.op("dve", (lambda c: lambda e: e.scalar_tensor_tensor(d_bf[:, c, Tp:Tt], wsum_s[:, c, :],
                                                                       vcol(V_INVW + c), xa_s[:, c, :], ALU.mult,
                                                                       ALU.subtract))(c),
                     reads=[WSUMS, XAS, CONST], writes=[DBF[c]])
        for c in range(2):
            pt, pb = ps_next()
            mm_group([(pt[:, 0:Tt], poolw_bf[:, (l * 2 + c) * 128:(l * 2 + c + 1) * 128], d_bf[:, c, 0:Tt], True,
                       True)], [DBF[c], C2["poolw_bf"]], pb)
            P.op("act", (lambda pt, c: lambda e: e.activation(out=ycat[:, c, 0:Tt], in_=pt[:, 0:Tt],
                                                              func=AF.Identity, scale=vcol(vb + V_PSC + c)))(pt, c),
                 reads=[pb, CONST], writes=[YC[c]])

        if DBG:
            P.op("pool", lambda e: e.dma_start(out=dbg_d[f"dbg_ycat{l}"][:, :, t0:t0 + Tp], in_=ycat[:, :, 0:Tp]),
                 reads=YC, chan=ch_dbg)
        for j in range(4):
            W, WB_ = next_blk(l, "wout", j)
            for hf in range(2):
                oc = 2 * j + hf
                pt, pb = ps_next()
                mm_group([(pt[:, 0:Tt], W[:, kc * 256 + hf * 128: kc * 256 + hf * 128 + 128], ycat[:, kc, 0:Tt],
                           kc == 0, kc == 7) for kc in range(8)], WB_ + YC, pb)
                resid_evac(l, pt, pb, oc, 16, 1, Tp, Ts)
            do_ada(("wout", j))

        if DBG:
            P.op("pool", (lambda xc: lambda e: e.dma_start(out=dbg_d[f"dbg_xmix{l}"][:, :, t0:t0 + Tp],
                                                          in_=xc[:, :, 0:Tp]))(CUR["x"]),
                 reads=CUR["XB"], chan=ch_dbg)
        if first:
            make_acoef(l, 1)
        norm_mod(l, 1, Tp, Ts)
        nxt = passes.index((ti, tile, l)) + 1
        conv_todo = list(CONV_ITEMS) if nxt < len(passes) else []
        nxt_l = passes[nxt][2] if nxt < len(passes) else 0

        def ffn_pair(j, slot0, firstgrp):
            W1, W1B = next_blk(l, "ff1", j)
            W3, W3B = next_blk(l, "ff3", j)
            for f in range(2):
                slot = slot0 + f
                p1, p1b = ps_next()
                p3, p3b = ps_next()
                mm1 = [(p1[:, 0:Tt], W1[:, kc * 256 + f * 128: kc * 256 + f * 128 + 128], hT[:, kc, 0:Tt],
                        kc == 0, kc == 7) for kc in range(8)]
                if firstgrp and f == 0:
                    mm_group(mm1, W1B + HBS, p1b, per=[[HB[kc]] for kc in range(8)])
                else:
                    mm_group(mm1, W1B + HBA, p1b)
                mm_group([(p3[:, 0:Tt], W3[:, kc * 256 + f * 128: kc * 256 + f * 128 + 128], hT[:, kc, 0:Tt],
                           kc == 0, kc == 7) for kc in range(8)], W3B + HBA, p3b)
                s_, sb_ = tmp_next()
                P.op("act", (lambda p1, s_: lambda e: e.activation(out=s_[:, 0:Tt], in_=p1[:, 0:Tt],
                                                                   func=AF.Silu))(p1, s_),
                     reads=[p1b], writes=[sb_])
                P.op("dve", (lambda p3, s_, slot: lambda e: e.tensor_tensor(g_ffn[:, slot, 0:Tt], s_[:, 0:Tt],
                                                                           p3[:, 0:Tt], ALU.mult))(p3, s_, slot),
                     reads=[p3b, sb_], writes=[GF[slot]])
                build_convD(nxt_l, conv_todo[:3])
                del conv_todo[:3]

        def ffn_down(half, slots):
            for dm in range(8):
                W2, W2B = next_blk(l, "ff2", half * 8 + dm)
                pt, pb = ps_next()
                n = len(slots)
                mm_group([(pt[:, 0:Tt], W2[:, i * 128:(i + 1) * 128], g_ffn[:, slots[i], 0:Tt], i == 0, i == n - 1)
                          for i in range(n)], W2B + [GF[sl] for sl in slots], pb)
                resid_evac(l, pt, pb, dm, 40, 3, Tp, Ts)

        for j in range(0, 5):
            ffn_pair(j, 2 * j, j == 0)
            do_ada(("pair", j))
        ffn_pair(5, 10, False)
        do_ada(("pair", 5))
        ffn_down(0, list(range(10)))
        for j in range(6, 11):
            ffn_pair(j, 2 * (j - 6), False)
            do_ada(("pair", j))
        ffn_down(1, [10, 11] + list(range(10)))
        build_convD(nxt_l, conv_todo)

        if DBG and l == 0:
            P.op("pool", (lambda xc: lambda e: e.dma_start(out=dbg_d["dbg_xffn0"][:, :, t0:t0 + Tp],
                                                          in_=xc[:, :, 0:Tp]))(CUR["x"]),
                 reads=CUR["XB"], chan=ch_dbg)

    passes = [(ti, tile, l) for ti, tile in enumerate(tiles) for l in range(DEPTH)]
    for (ti, tile, l) in passes:
        run_pass(ti, tile, l)
    emit_final(len(tiles) - 1)
    assert wstate["used"] == len(seq), (wstate, len(seq))

    esems = {e: es.enter_context(nc.semaphore("sem_" + e)) for e in ("pe", "act", "dve", "pool", "sp")}
    for c in P.chans:
        c.sem = es.enter_context(nc.semaphore("ch_" + c.name))
    with nc.allow_non_contiguous_dma(reason="small strided state/output tiles"):
        block = es.enter_context(nc.Block())
        P.emit(nc, block, esems)
        es.close()
    return nc


def _fm(v):
    sh = v.shape
    n = sh[-1] // 128
    r = v.reshape(sh[:-1] + (n, 128))
    return np.ascontiguousarray(np.moveaxis(r, -1, 0))


def _blocksA(W, nblk):
    r = W.reshape(8, 128, nblk, 256)
    return np.ascontiguousarray(r.transpose(2, 1, 0, 3)).reshape(nblk, 128, 2048)


def _host_layout(inp):
    f = lambda k: np.asarray(inp[k], dtype=np.float32)
    x_prompt, x_sample, c_prompt, c_sample = f("x_prompt"), f("x_sample"), f("c_prompt"), f("c_sample")
    state_pool, state_conv = f("state_pool"), f("state_conv")
    shared = {}
    for l in range(DEPTH):
        wA = np.concatenate([
            _blocksA(f("w_ada")[l], 24), _blocksA(f("w_in")[l], 7), _blocksA(f("w_out")[l], 4),
            _blocksA(f("w_ff1")[l], 11), _blocksA(f("w_ff3")[l], 11)], axis=0)
        shared[f"wA{l}"] = wA
        w2 = f("w_ff2")[l].reshape(NFC, 128, 8, 128)
        wB = np.zeros((16, 128, 1536), np.float32)
        fc0 = 0
        for half in range(2):
            nfc = HALF_FC[half]
            for dm in range(8):
                blk = w2[fc0:fc0 + nfc, :, dm, :].transpose(1, 0, 2).reshape(128, nfc * 128)
                wB[half * 8 + dm, :, 0:nfc * 128] = blk
            fc0 += nfc
        shared[f"wB{l}"] = wB
    vecs = np.zeros((128, NV), np.float32)
    for l in range(DEPTH):
        b = l * VL
        vecs[:, b + V_N1G:b + V_N1G + 8] = _fm(f("norm1_g")[l])
        vecs[:, b + V_N2G:b + V_N2G + 8] = _fm(f("norm2_g")[l])
        vecs[:, b + V_BADA:b + V_BADA + 48] = _fm(f("b_ada")[l])
        vecs[:, b + V_PSC:b + V_PSC + 2] = _fm(f("pool_scale")[l])
        vecs[:, b + V_CB:b + V_CB + 2] = _fm(f("conv_b")[l])
        vecs[:, b + V_CLG:b + V_CLG + 2] = _fm(f("conv_ln_g")[l])
        vecs[:, b + V_CLB:b + V_CLB + 2] = _fm(f("conv_ln_b")[l])
        cw = _fm(f("conv_w")[l])
        vecs[:, b + V_CW:b + V_CW + 62] = cw.transpose(0, 2, 1).reshape(128, 62)
    vecs[:, V_FING:V_FING + 8] = _fm(f("final_g"))
    wpp = np.array([[POOL_W[2 * c + (p // 64)] for c in range(2)] for p in range(128)], np.float32)
    vecs[:, V_INVW:V_INVW + 2] = 1.0 / wpp
    shared["vecs"] = vecs
    r16 = np.zeros((128, 2, 16), np.float32)
    for t in range(16):
        r16[:, :, t] = 1.0 / np.minimum(t + 1, wpp)
    shared["r16"] = r16
    poolw = np.zeros((128, DEPTH, 2, 128), np.float32)
    pw = f("pool_w")
    for l in range(DEPTH):
        for g in range(4):
            c, hf = g // 2, g % 2
            poolw[hf * 64:(hf + 1) * 64, l, c, hf * 64:(hf + 1) * 64] = pw[l, g]
    shared["poolw"] = poolw.reshape(128, -1)
    shared["gws"] = np.ascontiguousarray(f("gmlp_ws").transpose(3, 0, 1, 2)).reshape(128, -1)
    shared["maskT"] = np.triu(np.ones((128, 128), np.float32))
    shared["ident"] = np.eye(128, dtype=np.float32)
    bs = f("gmlp_bs")
    bsr = bs.reshape(DEPTH, 4, 2, 128).transpose(2, 0, 1, 3)
    shared["bsr"] = np.ascontiguousarray(bsr).reshape(2, -1)
    bsrs = np.repeat(bsr[:, :, :, 0:1], NS, axis=3)
    shared["bsrs"] = np.ascontiguousarray(bsrs).reshape(2, -1)
    sel = np.zeros((2, 128), np.float32)
    sel[0, :64] = 1
    sel[1, 64:] = 1
    shared["sel"] = sel
    ws00 = f("gmlp_ws")[:, :, 0, 0].reshape(1, DEPTH * 8)
    shared["ws00"] = np.ascontiguousarray(np.repeat(ws00, NS, axis=0))
    lngb = np.stack([f("gmlp_ln_g"), f("gmlp_ln_b")], axis=1)
    shared["lngb"] = np.ascontiguousarray(np.broadcast_to(lngb[:, None], (DEPTH, 128, 2, 512)))
    maps = []
    for i in range(NCORES):
        m = dict(shared)
        m["xT"] = np.ascontiguousarray(x_prompt[i].reshape(SEQ, 8, 128).transpose(2, 1, 0))
        xs = x_sample[i * NS:(i + 1) * NS, 0, :]
        m["xsT"] = np.ascontiguousarray(xs.reshape(NS, 8, 128).transpose(2, 1, 0))
        cc = np.concatenate([c_prompt[i:i + 1], c_sample[i * NS:(i + 1) * NS]], axis=0)
        m["cT"] = np.ascontiguousarray(cc.reshape(17, 8, 128).transpose(2, 1, 0))
        sp = state_pool[:, i * NS:(i + 1) * NS]
        m["spool"] = np.ascontiguousarray(sp.reshape(DEPTH, NS, 15, 2, 128).transpose(0, 4, 3, 1, 2))
        sc = state_conv[:, i * NS:(i + 1) * NS]
        m["sconv"] = np.ascontiguousarray(sc.reshape(DEPTH, NS, 30, 2, 128).transpose(0, 4, 3, 1, 2))
        maps.append(m)
    return maps


_NC_CACHE = {}


def kernel(**inputs):
    maps = _host_layout(inputs)
    if "nc" not in _NC_CACHE:
        _NC_CACHE["nc"] = build_nc()
    nc = _NC_CACHE["nc"]
    res = run_bass_kernel_spmd(nc, maps, core_ids=list(range(NCORES)))
    R = res.results
    if os.environ.get("MK_DBG"):
        _NC_CACHE["dbg"] = {k: R[0][k] for k in R[0] if k.startswith("dbg_")}
    y_prompt = np.stack([R[i]["yT"].transpose(2, 1, 0).reshape(SEQ, D) for i in range(NCORES)])
    y_sample = np.concatenate([R[i]["ysT"].transpose(2, 1, 0).reshape(NS, 1, D) for i in range(NCORES)])
    npp = np.stack([R[i]["npp"].transpose(0, 3, 2, 1).reshape(DEPTH, 15, 256) for i in range(NCORES)], axis=1)
    ncp = np.stack([R[i]["ncp"].transpose(0, 3, 2, 1).reshape(DEPTH, 30, 256) for i in range(NCORES)], axis=1)
    nps = np.concatenate([R[i]["nps"].transpose(0, 3, 4, 2, 1).reshape(DEPTH, NS, 15, 256) for i in range(NCORES)],
                         axis=1)
    ncs = np.concatenate([R[i]["ncs"].transpose(0, 3, 4, 2, 1).reshape(DEPTH, NS, 30, 256) for i in range(NCORES)],
                         axis=1)
    nvs = np.concatenate([R[i]["nvs"].reshape(DEPTH, NS, 1, 512) for i in range(NCORES)], axis=1)
    f32 = lambda a: np.ascontiguousarray(a, dtype=np.float32)
    return (f32(y_prompt), f32(y_sample), f32(npp), f32(ncp), f32(nps), f32(ncs), f32(nvs))
```
